# Optimizing a Trainium2 kernel written in Bass

```python
import jax, jax.numpy as jnp
from jax import lax
import numpy as np

D_MODEL = 1024
BATCH = 4
SEQ = 4096
DEPTH = 1

N_META = 16
POOL_WIDTH = D_MODEL // 2
POOL_WINDOWS = (2, 4, 8, 16)
POOL_GROUP = POOL_WIDTH // len(POOL_WINDOWS)
N_HEADS = 8
HEAD_DIM = 64
ATTN_WIDTH = N_HEADS * HEAD_DIM
Q_BLOCK = 128
RMS_EPS = 1e-6
IN_SIZES = (POOL_WIDTH, POOL_WIDTH, ATTN_WIDTH, ATTN_WIDTH, ATTN_WIDTH, ATTN_WIDTH, N_HEADS, D_MODEL, D_MODEL)
N_IN = 2 * POOL_WIDTH + 4 * ATTN_WIDTH + N_HEADS + 2 * D_MODEL

kernel_name = "gated_pool_forgetting_attn_hybrid"


def _split_points():
    pts, acc = [], 0
    for s in IN_SIZES[:-1]:
        acc += s
        pts.append(acc)
    return pts


def rmsnorm(x, g):
    xf = x.astype(jnp.float32)
    xf = xf * lax.rsqrt(jnp.mean(xf * xf, axis=-1, keepdims=True) + RMS_EPS)
    return (xf * g.astype(jnp.float32)).astype(x.dtype)


def causal_multiscale_pool(u, pool_w, pool_scale):
    B, L, _ = u.shape
    groups = jnp.split(u, len(POOL_WINDOWS), axis=-1)
    pos = jnp.arange(L, dtype=jnp.float32)[:, None]
    pooled = []
    for w, ug in zip(POOL_WINDOWS, groups):
        uf = ug.astype(jnp.float32)
        c = jnp.cumsum(uf, axis=1)
        c_prev = jnp.pad(c, ((0, 0), (w, 0), (0, 0)))[:, :L]
        count = jnp.minimum(pos + 1.0, float(w))
        pooled.append((c - c_prev) / count - uf)
    p = jnp.stack(pooled, axis=2).astype(u.dtype)
    y = jnp.einsum('blgc,gcd->blgd', p, pool_w)
    return y.reshape(B, L, POOL_WIDTH) * pool_scale


def forgetting_attention(q, k, v, log_f):
    B, L, H, Dh = q.shape
    pad = (Q_BLOCK - L % Q_BLOCK) % Q_BLOCK
    padw = ((0, 0), (pad, 0), (0, 0), (0, 0))
    qp, kp, vp = jnp.pad(q, padw), jnp.pad(k, padw), jnp.pad(v, padw)
    c = jnp.cumsum(log_f, axis=1)
    c = jnp.transpose(jnp.pad(c, ((0, 0), (pad, 0), (0, 0))), (0, 2, 1))
    Lp = L + pad
    scale = 1.0 / np.sqrt(HEAD_DIM)
    outs = []
    for i in range(Lp // Q_BLOCK):
        q0, q1 = i * Q_BLOCK, (i + 1) * Q_BLOCK
        qb, kb, vb = qp[:, q0:q1], kp[:, :q1], vp[:, :q1]
        s = jnp.einsum('bqhd,bkhd->bhqk', qb, kb).astype(jnp.float32) * scale
        s = s + c[:, :, q0:q1, None] - c[:, :, None, :q1]
        q_idx = jnp.arange(q0, q1)[:, None]
        k_idx = jnp.arange(q1)[None, :]
        valid = (k_idx <= q_idx) & ((k_idx >= pad) | (k_idx == q_idx))
        s = jnp.where(valid, s, -jnp.inf)
        p = jax.nn.softmax(s, axis=-1)
        outs.append(jnp.einsum('bhqk,bkhd->bqhd', p.astype(vb.dtype), vb))
    o = jnp.concatenate(outs, axis=1)[:, pad:]
    return o


def hybrid_layer(x, norm_g, w_in, b_forget, pool_w, pool_scale, w_up_pool, w_up_attn, w_out):
    B, L, _ = x.shape
    h = rmsnorm(x, norm_g)
    proj = jnp.einsum('bld,dn->bln', h, w_in)
    u_pool, z_pool, q, k, v, z_attn, f_logit, g_pool, g_attn = jnp.split(proj, _split_points(), axis=-1)
    y_pool = causal_multiscale_pool(u_pool, pool_w, pool_scale) * jax.nn.silu(z_pool)
    q = q.reshape(B, L, N_HEADS, HEAD_DIM)
    k = k.reshape(B, L, N_HEADS, HEAD_DIM)
    v = v.reshape(B, L, N_HEADS, HEAD_DIM)
    log_f = jax.nn.log_sigmoid((f_logit + b_forget).astype(jnp.float32))
    y_attn = forgetting_attention(q, k, v, log_f).reshape(B, L, ATTN_WIDTH) * jax.nn.silu(z_attn)
    merged = (jax.nn.sigmoid(g_pool) * jnp.einsum('blc,cd->bld', y_pool, w_up_pool)
              + jax.nn.sigmoid(g_attn) * jnp.einsum('blc,cd->bld', y_attn, w_up_attn))
    return x + jnp.einsum('bld,de->ble', merged, w_out)


def setup_inputs(seed: int = 0) -> dict:
    key = jax.random.key(seed)
    ks = jax.random.split(key, 12)
    f0 = sum(IN_SIZES[:6])
    x = jax.random.normal(ks[0], (BATCH, SEQ, D_MODEL), jnp.float32)
    meta_tokens = jax.random.normal(ks[1], (N_META, D_MODEL), jnp.float32)
    norm_g = 1.0 + 0.02 * jax.random.normal(ks[2], (DEPTH, D_MODEL), jnp.float32)
    w_in = jax.random.normal(ks[3], (DEPTH, D_MODEL, N_IN), jnp.float32) * D_MODEL ** -0.5
    w_in = w_in.at[:, :, f0:f0 + N_HEADS].multiply(0.1)
    b_forget = (jnp.linspace(1.0, 6.0, N_HEADS, dtype=jnp.float32)[None, :]
                + 0.1 * jax.random.normal(ks[4], (DEPTH, N_HEADS), jnp.float32))
    pool_w = jax.random.normal(ks[5], (DEPTH, len(POOL_WINDOWS), POOL_GROUP, POOL_GROUP), jnp.float32) * POOL_GROUP ** -0.5
    pool_scale = 1.0 + 0.1 * jax.random.normal(ks[6], (DEPTH, POOL_WIDTH), jnp.float32)
    w_up_pool = jax.random.normal(ks[7], (DEPTH, POOL_WIDTH, D_MODEL), jnp.float32) * POOL_WIDTH ** -0.5
    w_up_attn = jax.random.normal(ks[8], (DEPTH, ATTN_WIDTH, D_MODEL), jnp.float32) * ATTN_WIDTH ** -0.5
    w_out = jax.random.normal(ks[9], (DEPTH, D_MODEL, D_MODEL), jnp.float32) * D_MODEL ** -0.5
    final_norm_g = 1.0 + 0.02 * jax.random.normal(ks[10], (D_MODEL,), jnp.float32)
    return {"x": x, "meta_tokens": meta_tokens, "norm_g": norm_g, "w_in": w_in,
            "b_forget": b_forget, "pool_w": pool_w, "pool_scale": pool_scale,
            "w_up_pool": w_up_pool, "w_up_attn": w_up_attn, "w_out": w_out,
            "final_norm_g": final_norm_g}


def reference(x, meta_tokens, norm_g, w_in, b_forget, pool_w, pool_scale, w_up_pool, w_up_attn, w_out, final_norm_g):
    B = x.shape[0]
    meta = jnp.broadcast_to(meta_tokens.astype(x.dtype)[None], (B, N_META, D_MODEL))
    h = jnp.concatenate([meta, x], axis=1)
    for l in range(DEPTH):
        h = hybrid_layer(h, norm_g[l], w_in[l], b_forget[l], pool_w[l], pool_scale[l],
                         w_up_pool[l], w_up_attn[l], w_out[l])
    h = rmsnorm(h, final_norm_g)
    return h[:, N_META:]
```

```python
from contextlib import ExitStack
import numpy as np
import concourse.bass as bass
import concourse.mybir as mybir
from concourse.bass_utils import run_bass_kernel_spmd

F32 = mybir.dt.float32
BF16 = mybir.dt.bfloat16
ALU = mybir.AluOpType
ACT = mybir.ActivationFunctionType

import os
DEBUG = False
STOP = int(os.environ.get("KSTOP", "99"))
SUB = int(os.environ.get("KSUB", "99"))
NEG = -240000.0
EPS = 1e-6
NTOK = 4368
META0 = 4096
HALO0 = 4112
ENGS = ["tensor", "vector", "scalar", "gpsimd", "sync"]
EPOCH = 1000


class Prog:
    def __init__(self):
        self.ops = {e: [] for e in ENGS}
        self.last_writer = {}
        self.readers = {}
        self.all_ops = []
        self.dma_counts = {}

    def op(self, eng, fn, reads=(), writes=(), dma=None, extra=()):
        deps = list(extra)
        for b in reads:
            w = self.last_writer.get(b)
            if w is not None:
                deps.append(w)
        for b in writes:
            w = self.last_writer.get(b)
            if w is not None:
                deps.append(w)
            deps.extend(self.readers.get(b, ()))
        rec = dict(eng=eng, fn=fn, deps=[], dma=dma, sig=False, id=len(self.all_ops))
        seen = set()
        raw = set(id(self.last_writer.get(b)) for b in reads)
        for d in deps:
            if d["id"] in seen:
                continue
            seen.add(d["id"])
            if d["eng"] == eng and d["dma"] is None and dma is None and eng != "gpsimd":
                if id(d) not in raw:
                    continue
            rec["deps"].append(d)
            d["sig"] = True
        if dma is not None:
            self.dma_counts[dma] = self.dma_counts.get(dma, 0) + 1
            rec["dma_n"] = self.dma_counts[dma]
        for b in reads:
            self.readers.setdefault(b, []).append(rec)
        for b in writes:
            self.last_writer[b] = rec
            self.readers[b] = []
        self.ops[eng].append(rec)
        self.all_ops.append(rec)
        return rec

    def wait_only(self, eng, recs):
        rec = dict(eng=eng, fn=None, deps=list(recs), dma=None, sig=False, id=len(self.all_ops))
        for d in recs:
            d["sig"] = True
        self.ops[eng].append(rec)
        self.all_ops.append(rec)
        return rec

    def fence_set(self):
        recs = []
        for e in ENGS:
            for r in reversed(self.ops[e]):
                if r["fn"] is not None and r["dma"] is None:
                    recs.append(r)
                    break
        last = {}
        for r in self.all_ops:
            if r["dma"] is not None:
                last[r["dma"]] = r
        recs.extend(last.values())
        return recs

    def emit(self, nc):
        nsem = {}
        for e in ENGS:
            n = 0
            for r in self.ops[e]:
                if r["dma"] is None and r["sig"] and r["fn"] is not None:
                    n += 1
                    r["sig_n"] = n
            nsem[e] = (n + EPOCH - 1) // EPOCH
        with ExitStack() as es:
            esem = {e: [es.enter_context(nc.semaphore(f"s_{e}_{i}")) for i in range(nsem[e])] for e in ENGS}
            dsem = {k: es.enter_context(nc.semaphore(f"d_{k}")) for k in self.dma_counts}
            block = es.enter_context(nc.Block())

            def make(e):
                def body(engine):
                    waited = {}
                    for r in self.ops[e]:
                        for d in r["deps"]:
                            if d["dma"] is not None:
                                sem, val, key = dsem[d["dma"]], 16 * d["dma_n"], ("d", d["dma"])
                            else:
                                ep, v = divmod(d["sig_n"] - 1, EPOCH)
                                sem, val, key = esem[d["eng"]][ep], v + 1, (d["eng"], ep)
                            if waited.get(key, 0) >= val:
                                continue
                            waited[key] = val
                            engine.wait_ge(sem, val)
                        if r["fn"] is None:
                            continue
                        inst = r["fn"](engine)
                        if r["dma"] is not None:
                            inst.then_inc(dsem[r["dma"]], 16)
                        elif r["sig"]:
                            ep, v = divmod(r["sig_n"] - 1, EPOCH)
                            inst.then_inc(esem[e][ep], 1)
                return body

            for e in ENGS:
                if self.ops[e]:
                    getattr(block, e)(make(e))


def build_nc():
    nc = bass.Bass("TRN2", target_bir_lowering=False)
    D = lambda name, shape, kind="ExternalInput": nc.dram_tensor(name, shape, F32, kind=kind).ap()
    xT = D("xT", [1024, NTOK])
    x_own = D("x_own", [2048, 1024])
    w_in = D("w_in", [1024, 5128])
    pool_w = D("pool_w", [4, 128, 128])
    w_up_pool = D("w_up_pool", [512, 1024])
    w_up_attn = D("w_up_attn", [512, 1024])
    w_out = D("w_out", [1024, 1024])
    ng = D("ng", [128, 8])
    pscale = D("pscale", [128, 4])
    bfg = D("bfg", [8])
    fng = D("fng", [1024])
    pm = D("pm", [33, 33])
    masks = D("masks", [128, 384])
    out = D("out", [2048, 1024], kind="ExternalOutput")
    if DEBUG:
        dbg_kt = D("dbg_kt", [128, 4 * 4112], kind="ExternalOutput")
        dbg_nc = D("dbg_nc", [128, 264], kind="ExternalOutput")
        dbg_oatt = D("dbg_oatt", [128, 4 * 2048], kind="ExternalOutput")
        dbg_qt = D("dbg_qt", [128, 4 * 2048], kind="ExternalOutput")
        dbg_vp = D("dbg_vp", [128, 33 * 768], kind="ExternalOutput")

    xT_v = xT.rearrange("(k p) n -> p k n", p=128)
    w_in_v = w_in.rearrange("(k p) n -> p k n", p=128)

    P = Prog()
    with ExitStack() as es:
        ARENA = 106000
        arena = es.enter_context(nc.sbuf_tensor("arena", [128, ARENA], BF16))
        ps = [es.enter_context(nc.psum_tensor(f"ps{i}", [128, 512], F32)) for i in range(8)]

        def view(off, nfree, dt, parts=128):
            assert off % 4 == 0
            nb = nfree * (4 if dt == F32 else 2)
            a = arena[0:parts, off // 2:(off + nb) // 2]
            if dt == F32:
                a = a.bitcast(F32)
            return a

        class Bump:
            def __init__(self, start, end):
                self.o, self.end = start, end

            def __call__(self, nfree, dt, parts=128):
                nb = nfree * (4 if dt == F32 else 2)
                nb = (nb + 63) // 64 * 64
                v = view(self.o, nfree, dt, parts)
                self.o += nb
                assert self.o <= self.end, (self.o, self.end)
                return v

        C0, C1 = 0, 8192
        O0, O1 = C1, C1 + 16384
        T0, T1 = O1, O1 + 57344
        A0, A1 = T1, T1 + 101056
        D0, D1 = A1, ARENA * 2
        cb = Bump(C0, C1)
        ident_bf = cb(128, BF16)
        ones_bf = cb(128, BF16)
        ones_f = cb(128, F32)
        tri_f = cb(128, F32)
        masks_bf = cb(384, BF16)
        g1 = cb(8, F32)
        hps = cb(4, F32)
        bfb = cb(8, F32)
        pm_sb = cb(33, F32, parts=33)
        fng_b = cb(1024, F32)
        eps_c = cb(1, F32)
        act_warm = cb(1, F32)
        ident_f = cb(128, F32)
        OATT = view(O0, 4 * 2048, BF16).rearrange("p (c n) -> p c n", c=4)

        ab = Bump(A0, A1)
        KT = ab(4 * 4112, BF16).rearrange("p (c n) -> p c n", c=4)
        QT = ab(4 * 2048, BF16).rearrange("p (c n) -> p c n", c=4)
        VP = ab(33 * 768, BF16).rearrange("p (b c n) -> p b c n", b=33, c=4)
        NC = ab(264, F32).rearrange("p (h b) -> p h b", h=8)

        tb = Bump(T0, T1)
        W1q = tb(8 * 512, BF16).rearrange("p (k n) -> p k n", k=8)
        W1k = tb(8 * 512, BF16).rearrange("p (k n) -> p k n", k=8)
        W1v = tb(8 * 512, BF16).rearrange("p (k n) -> p k n", k=8)
        W1f = tb(8 * 8, BF16).rearrange("p (k n) -> p k n", k=8)
        xs = tb(8 * 512, F32).rearrange("p (k n) -> p k n", k=8)
        xb = tb(8 * 512, BF16).rearrange("p (k n) -> p k n", k=8)
        rstd_b = tb(528, F32)
        NLF = tb(264, F32).rearrange("p (h b) -> p h b", h=8)
        ft0 = tb(32, F32).rearrange("p (b h) -> p b h", b=4)
        ft1 = tb(32, F32).rearrange("p (b h) -> p b h", b=4)
        ST = tb(8, F32, parts=33)
        STrep = tb(8 * 128, F32, parts=33).rearrange("p (h n) -> p h n", h=8)

        db = Bump(D0, D1)
        PT = [[db(512, BF16) for _ in range(2)] for _ in range(2)]
        Osb = [db(512, F32) for _ in range(2)]
        Dsb = [db(512, F32) for _ in range(2)]
        negones = db(512, F32)
        Rp = [db(4 * 512, BF16).rearrange("p (c n) -> p c n", c=4) for _ in range(2)]
        ZR = [db(16 * 65, BF16).rearrange("p (s c n) -> p s c n", s=4, c=4) for _ in range(2)]
        onesP = db(128, BF16)

        def mm(o, lhsT, rhs, start, stop, reads, writes, extra=()):
            return P.op("tensor", lambda e: e.matmul(o, lhsT=lhsT, rhs=rhs, start=start, stop=stop), reads, writes, extra=extra)

        def act(o, i, func, reads, writes, bias=None, scale=None, accum=None):
            kw = {}
            if bias is not None:
                kw["bias"] = bias
            if scale is not None:
                kw["scale"] = scale
            if accum is not None:
                kw["accum_out"] = accum
            return P.op("scalar", lambda e: e.activation(out=o, in_=i, func=func, **kw), reads, writes)

        def copy(eng, o, i, reads, writes):
            if eng == "scalar":
                return P.op("scalar", lambda e: e.activation(out=o, in_=i, func=ACT.Copy), reads, writes)
            return P.op(eng, lambda e: e.tensor_copy(out=o, in_=i), reads, writes)

        def tt(eng, o, a, b, op, reads, writes):
            return P.op(eng, lambda e: e.tensor_tensor(out=o, in0=a, in1=b, op=op), reads, writes)

        def stt(eng, o, a, s, b, op0, op1, reads, writes):
            return P.op(eng, lambda e: e.scalar_tensor_tensor(out=o, in0=a, scalar=s, in1=b, op0=op0, op1=op1), reads, writes)

        def ts(eng, o, a, s1, s2, op0, op1, reads, writes):
            if s2 is None:
                return P.op(eng, lambda e: e.tensor_scalar(out=o, in0=a, scalar1=s1, scalar2=None, op0=op0), reads, writes)
            return P.op(eng, lambda e: e.tensor_scalar(out=o, in0=a, scalar1=s1, scalar2=s2, op0=op0, op1=op1), reads, writes)

        def memset(eng, o, val, writes):
            return P.op(eng, lambda e: e.memset(o, val), (), writes)

        def dma(q, o, i, reads, writes, key, extra=()):
            return P.op(q, lambda e: e.dma_start(out=o, in_=i), reads, writes, dma=key, extra=extra)

        dma("sync", g1, ng, [], ["g1"], "c_g1")
        dma("sync", bfb, bfg.partition_broadcast(128), [], ["bfb"], "c_bfb")
        memset("vector", ones_bf, 1.0, ["ones_bf"])
        memset("vector", ones_f, 1.0, ["ones_f"])
        memset("vector", eps_c, EPS, ["eps_c"])
        act(act_warm, eps_c, ACT.Ln, ["eps_c"], ["act_warm"])
        memset("vector", NLF, 0.0, ["NLF"])
        memset("gpsimd", tri_f, 1.0, ["tri_f"])
        P.op("gpsimd", lambda e: e.affine_select(out=tri_f, in_=tri_f, pattern=[[1, 128]], compare_op=ALU.is_ge,
                                                 fill=0.0, base=0, channel_multiplier=-1), ["tri_f"], ["tri_f"])
        memset("gpsimd", ident_f, 1.0, ["ident_f"])
        P.op("gpsimd", lambda e: e.affine_select(out=ident_f, in_=ident_f, pattern=[[1, 128]], compare_op=ALU.is_ge,
                                                 fill=0.0, base=0, channel_multiplier=-1), ["ident_f"], ["ident_f"])
        P.op("gpsimd", lambda e: e.affine_select(out=ident_f, in_=ident_f, pattern=[[-1, 128]], compare_op=ALU.is_ge,
                                                 fill=0.0, base=0, channel_multiplier=1), ["ident_f"], ["ident_f"])

        pb = [0]

        bank_first = []

        reserved = set()

        def bank():
            if bank_first:
                return bank_first.pop(0)
            while True:
                i = pb[0]
                pb[0] = (pb[0] + 1) % 8
                if i not in reserved:
                    return i

        def issue_x(xs_t, col0, n, xsk, dkey, sub=(), extra=()):
            r = dma("sync", xs_t[:, :, 0:n], xT_v[:, :, col0:col0 + n], [], [xsk], dkey, extra=extra)
            for (c0, cn, off) in sub:
                r = dma("sync", xs_t[:, :, off:off + cn], xT_v[:, :, c0:c0 + cn], [], [xsk], dkey, extra=extra)
            return r

        def norm_sq(xs_t, xb_t, ntot, xsk, xbk):
            for h_ in range(2):
                act(xb_t[:, 4 * h_:4 * h_ + 4, 0:ntot], xs_t[:, 4 * h_:4 * h_ + 4, 0:ntot], ACT.Square, [xsk],
                    [(xbk, k_) for k_ in range(4 * h_, 4 * h_ + 4)])

        def norm_x(xs_t, xb_t, rstd_t, ntot, xsk, xbk, rk, sq=True):
            if sq:
                norm_sq(xs_t, xb_t, ntot, xsk, xbk)
            done = 0
            while done < ntot:
                cn = min(512, ntot - done)
                b = bank()
                for k in range(8):
                    mm(ps[b][:, 0:cn], ones_bf, xb_t[:, k, done:done + cn], k == 0, k == 7, [(xbk, k), "ones_bf"], ["ps%d" % b])
                act(rstd_t[:, done:done + cn], ps[b][:, 0:cn], ACT.Ln, ["ps%d" % b, "eps_c"], [rk], bias=eps_c, scale=1.0 / 1024)
                done += cn
            act(rstd_t[:, 0:ntot], rstd_t[:, 0:ntot], ACT.Exp, [rk], [rk], scale=-0.5)
            for k in range(8):
                stt("vector", xb_t[:, k, 0:ntot], xs_t[:, k, 0:ntot], g1[:, k:k + 1], rstd_t[:, 0:ntot],
                    ALU.mult, ALU.mult, [xsk, rk, "g1"], [(xbk, k)])

        def load_norm(xs_t, xb_t, rstd_t, col0, n, tag, sub=()):
            issue_x(xs_t, col0, n, tag + "xs", tag + "x", sub)
            norm_x(xs_t, xb_t, rstd_t, n + sum(s_[1] for s_ in sub), tag + "xs", tag + "xb", tag + "rstd")

        tiles = []
        for t in range(4):
            tiles.append((2048 + t * 512, 512, 16 + 4 * t, False, -1))
        for t in range(4):
            tiles.append((t * 512, 512, 4 * t, True, t))
        ev = [0]

        def evac_eng():
            ev[0] += 1
            return "scalar" if ev[0] % 2 == 0 else "vector"

        if STOP < 1:
            tiles = []
        elif STOP < 2:
            tiles = tiles[0:1]
        xs2 = view(D0, 8 * 528, F32).rearrange("p (k n) -> p k n", k=8)
        xb2 = view(D0 + 16896, 8 * 528, BF16).rearrange("p (k n) -> p k n", k=8)
        XS = [xs2, xs]
        XB = [xb2, xb]
        NTL = len(tiles)

        def p1_n(ti):
            return tiles[ti][1] + (16 if ti == 0 else 0)

        def p1_issue(ti, extra=()):
            return issue_x(XS[ti % 2], tiles[ti][0], tiles[ti][1], "p1xs%d" % (ti % 2), "p1x%d" % (ti % 2),
                           sub=([(META0, 16, 512)] if ti == 0 else ()), extra=extra)

        def p1_sq(ti):
            norm_sq(XS[ti % 2], XB[ti % 2], p1_n(ti), "p1xs%d" % (ti % 2), "p1xb%d" % (ti % 2))

        def p1_norm(ti, sq=True):
            norm_x(XS[ti % 2], XB[ti % 2], rstd_b, p1_n(ti), "p1xs%d" % (ti % 2), "p1xb%d" % (ti % 2), "p1rstd", sq=sq)

        cs_bw = [None]

        def emit_cumsum(part):
            if part == 1:
                bw = cs_bw[0]
                for h in range(8):
                    mm(ps[bw][:, h * 33:(h + 1) * 33], STrep[:, h, :], pm_sb, False, h == 7, ["STrep", "pm"], ["ps%d" % bw])
                copy("vector", NC.rearrange("p h b -> p (h b)"), ps[bw][:, 0:264], ["ps%d" % bw], ["NC"])
                return
            bw = bank()
            cs_bw[0] = bw
            NLF2 = NLF.rearrange("p h b -> p (h b)")
            mm(ps[bw][:, 0:264], tri_f, NLF2, True, False, ["NLF", "tri_f"], ["ps%d" % bw])
            bs = bank()
            for h in range(8):
                mm(ps[bs][0:33, h:h + 1], NLF[:, h, :], ones_f[:, 0:1], True, True, ["NLF", "ones_f"], ["ps%d" % bs])
            copy("vector", ST, ps[bs][0:33, 0:8], ["ps%d" % bs], ["ST"])
            copy("vector", STrep, ST.unsqueeze(2).to_broadcast([33, 8, 128]), ["ST"], ["STrep"])

        MB = 3

        def rgen(T, part=0, pair=None):
            zr = ZR[T % 2]
            rp_ = Rp[T % 2]
            zk = "ZR%d" % (T % 2)
            rk = "Rp%d" % (T % 2)
            if part in (0, 1):
                ts("vector", zr[:, :, :, 0:65:64], NC[:, :, 4 * T:4 * T + 4].rearrange("p (c e) s -> p s c e", c=4), -8.0, None,
                   ALU.mult, ALU.bypass, ["NC"], [zk])
            if part == 1:
                return
            for p in (range(4) if pair is None else [pair]):
                for s_ in range(4):
                    mm(ps[MB][0:65, s_ * 128:(s_ + 1) * 128], zr[:, s_, p, :], ident_bf, True, True, [zk, "ident_bf"], ["ps%d" % MB])
                copy("vector", rp_[0:65, p, :], ps[MB][0:65, :], ["ps%d" % MB], [rk])

        def prep_attn_consts(extra):
            for i_, z in enumerate(ZR):
                P.op("gpsimd", lambda e_, z=z: e_.memset(z, 0.0), (), ["ZR%d" % i_], extra=extra)
            for i_, z in enumerate(Rp):
                P.op("gpsimd", lambda e_, z=z: e_.memset(z, 0.0), (), ["Rp%d" % i_], extra=extra)
            P.op("gpsimd", lambda e_: e_.memset(onesP, 0.0), (), ["onesP"], extra=extra)
            memset("gpsimd", negones, -1.0, ["negones"])
            memset("gpsimd", onesP[0:1, :], 1.0, ["onesP"])
            memset("gpsimd", onesP[64:65, :], 1.0, ["onesP"])

        r_x0 = p1_issue(0)
        r_wf = dma("gpsimd", W1f, w_in_v[:, :, 3072:3080], [], ["W1f"], "w1f", extra=[r_x0])
        r_wk = dma("gpsimd", W1k, w_in_v[:, :, 1536:2048], [], ["W1k"], "w1k", extra=[r_x0])
        memset("gpsimd", VP[:, :, :, 64:128], 1.0, ["VP"])
        dma("gpsimd", W1v, w_in_v[:, :, 2048:2560], [], ["W1v"], "w1v", extra=[r_wk])
        dma("sync", hps, pscale, [], ["hps"], "c_hps")
        dma("sync", pm_sb, pm, [], ["pm"], "c_pm")
        dma("sync", fng_b, fng.partition_broadcast(128), [], ["fng"], "c_fng")
        if NTL > 1:
            p1_issue(1, extra=[r_wk])
        dma("gpsimd", W1q, w_in_v[:, :, 1024:1536], [], ["W1q"], "w1q", extra=[r_wk])
        dma("gpsimd", masks_bf, masks, [], ["masks"], "c_masks")
        if NTL:
            p1_norm(0)
        fence7 = [None]
        rgen0_done = [False]

        def tile_parts(xbt, xbk, coff, n, lb0, col0, ot):
            nblk = (n + 127) // 128

            def f_path():
                bf_ = bank()
                for i in range(nblk):
                    m = min(128, n - i * 128)
                    for k in range(8):
                        mm(ps[bf_][0:m, i * 8:(i + 1) * 8], xbt[:, k, coff + i * 128:coff + i * 128 + m], W1f[:, k, :], k == 0, k == 7,
                           [(xbk, k), "W1f"], ["ps%d" % bf_])
                m = min(128, n)
                psf = ps[bf_][0:m, 0:nblk * 8].rearrange("p (b h) -> p b h", h=8)
                tt("vector", ft0[0:m, 0:nblk, :], psf, bfb[0:m, :].unsqueeze(1).to_broadcast([m, nblk, 8]), ALU.add,
                   ["ps%d" % bf_, "bfb"], ["ft0"])
                act(ft1[0:m, 0:nblk, :], ft0[0:m, 0:nblk, :], ACT.Exp, ["ft0"], ["ft1"], scale=-1.0)
                act(NLF[0:m, :, lb0:lb0 + nblk].rearrange("p h b -> p b h"), ft1[0:m, 0:nblk, :], ACT.Ln, ["ft1"], ["NLF"], bias=1.0)

            def k_proj():
                for p in range(4):
                    b = bank()
                    for k in range(8):
                        mm(ps[b][:, 0:n], W1k[:, k, p * 128:(p + 1) * 128], xbt[:, k, coff:coff + n], k == 0, k == 7, [(xbk, k), "W1k"], ["ps%d" % b])
                    copy(evac_eng(), KT[:, p, col0:col0 + n], ps[b][:, 0:n], ["ps%d" % b], [("KT", p, col0)])

            def v_proj():
                for i in range(nblk):
                    m = min(128, n - i * 128)
                    b = bank()
                    for k in range(8):
                        mm(ps[b][0:m, :], xbt[:, k, coff + i * 128:coff + i * 128 + m], W1v[:, k, :], k == 0, k == 7, [(xbk, k), "W1v"], ["ps%d" % b])
                    pv = ps[b][0:m, :].rearrange("p (c n) -> p c n", c=4)
                    ve = evac_eng()
                    copy(ve, VP[0:m, lb0 + i, :, 0:64], pv[:, :, 0:64], ["ps%d" % b, "VP"], [("VP", lb0 + i, 0)])
                    copy(ve, VP[0:m, lb0 + i, :, 128:192], pv[:, :, 64:128], ["ps%d" % b, "VP"], [("VP", lb0 + i, 1)])

            def q_proj():
                for p in range(4):
                    b = bank()
                    for k in range(8):
                        mm(ps[b][:, 0:n], W1q[:, k, p * 128:(p + 1) * 128], xbt[:, k, coff:coff + n], k == 0, k == 7, [(xbk, k), "W1q"], ["ps%d" % b])
                    copy(evac_eng(), QT[:, p, col0:col0 + n], ps[b][:, 0:n], ["ps%d" % b], [("QT", p, ot)])

            return f_path, k_proj, v_proj, q_proj

        for ti, (col0, n, lb0, own, ot) in enumerate(tiles):
            if ti == NTL - 1 and NTL == 8 and STOP >= 4:
                fence7[0] = P.fence_set()
                prep_attn_consts(fence7[0])
            xbt = XB[ti % 2]
            xbk = "p1xb%d" % (ti % 2)
            if ti + 1 < NTL:
                p1_sq(ti + 1)
            f_path, k_proj, v_proj, q_proj = tile_parts(xbt, xbk, 0, n, lb0, col0, ot)
            f_path()
            k_proj()
            if ti + 1 < NTL:
                p1_norm(ti + 1, sq=False)
            if ti + 2 < NTL:
                p1_issue(ti + 2)
            if ti == 0:
                mf, mk, mv, _ = tile_parts(xbt, xbk, 512, 16, 32, META0, -1)
                mf()
                mk()
            if ti == NTL - 1 and STOP >= 3:
                emit_cumsum(0)
            v_proj()
            if ti == 0:
                mv()
            if ti == NTL - 1 and STOP >= 3:
                emit_cumsum(1)
                if STOP >= 4 and fence7[0] is not None:
                    rgen(0)
                    rgen(1, part=1)
                    rgen0_done[0] = True
            if own:
                q_proj()
            if ti == 0:
                copy("vector", ident_bf, ident_f, ["ident_f"], ["ident_bf"])
                for g_ in range(4):
                    ts("vector", hps[:, g_:g_ + 1], hps[:, g_:g_ + 1], 1.0 / (2 ** (g_ + 1)), None, ALU.mult, ALU.bypass, ["hps"], ["hps"])

        if STOP < 3:
            fin = P.fence_set()
            P.wait_only("sync", fin)
            P.emit(nc)
            return nc
        ph1_fence = P.fence_set()

        W2 = view(T0, 8 * 3584, BF16).rearrange("p (k n) -> p k n", k=8)
        w2_srcs = [(0, 1024, 0), (2560, 512, 1024), (3080, 1024, 1536), (4104, 1024, 2560)]
        for wi, (c0, cn, d0) in enumerate(w2_srcs):
            dma("gpsimd", W2[:, :, d0:d0 + cn], w_in_v[:, :, c0:c0 + cn], [], [("W2", wi)], "w2_%d" % wi, extra=ph1_fence)

        def w2key(col):
            for wi, (c0, cn, d0) in enumerate(w2_srcs):
                if d0 <= col < d0 + cn:
                    return ("W2", wi)

        p3 = Bump(A0, D1)
        xs3 = p3(8 * 576, F32).rearrange("p (k n) -> p k n", k=8)
        sqk = [p3(576, BF16) for _ in range(2)]
        rstd3 = p3(576, F32)
        assert p3.o <= A0 + 3 * 8224
        WUP = p3(4 * 1024, BF16).rearrange("p (c n) -> p c n", c=4)
        WUA = p3(4 * 1024, BF16).rearrange("p (c n) -> p c n", c=4)
        WO = p3(8 * 1024, BF16).rearrange("p (k n) -> p k n", k=8)
        PW = p3(4 * 128, BF16).rearrange("p (g n) -> p g n", g=4)
        xb3 = p3(8 * 576, BF16).rearrange("p (k n) -> p k n", k=8)
        U = [p3(4 * 144, F32).rearrange("p (b n) -> p b n", b=4) for _ in range(4)]
        S1 = p3(4 * 144, F32).rearrange("p (b n) -> p b n", b=4)
        S2 = p3(4 * 144, F32).rearrange("p (b n) -> p b n", b=4)
        PLj = p3(2048, BF16)
        PL = [PLj[:, i_ * 512:(i_ + 1) * 512] for i_ in range(4)]
        YP = p3(4 * 512, BF16).rearrange("p (c n) -> p c n", c=4)
        YA = p3(4 * 512, BF16).rearrange("p (c n) -> p c n", c=4)
        TH = [p3(512, F32) for _ in range(2)]
        TTt = [p3(512, F32) for _ in range(2)]
        M1 = p3(512, F32)
        M2 = p3(512, F32)
        MG = p3(8 * 512, BF16).rearrange("p (k n) -> p k n", k=8)
        RES = [p3(1024, F32) for _ in range(4)]
        ss4 = p3(4, F32)

        thc = [0]

        def p3_issue(T, extra=()):
            issue_x(xs3, T * 512, 512, "p3xs", "p3x", sub=[(HALO0 + 64 * T, 64, 512)], extra=extra)

        def p3_rstd(T, banks=None, extra=(), on_dve=False, ks=range(8), do_sq=True, do_mm=True, fin=True):
            bA, bB = banks if banks is not None else (bank(), bank())
            for k in ks:
                sq = sqk[k % 2]
                if not do_sq:
                    pass
                elif on_dve:
                    P.op("vector", lambda e_, o=sq, i=xs3[:, k, :]: e_.tensor_tensor(out=o, in0=i, in1=i, op=ALU.mult), ["p3xs"], ["sqk%d" % (k % 2)], extra=extra)
                else:
                    P.op("scalar", lambda e_, o=sq, i=xs3[:, k, :]: e_.activation(out=o, in_=i, func=ACT.Square), ["p3xs"], ["sqk%d" % (k % 2)], extra=extra)
                if do_mm:
                    mm(ps[bA][:, 0:512], ones_bf, sq[:, 0:512], k == 0, k == 7, ["sqk%d" % (k % 2), "ones_bf"], ["ps%d" % bA])
                    mm(ps[bB][:, 0:64], ones_bf, sq[:, 512:576], k == 0, k == 7, ["sqk%d" % (k % 2), "ones_bf"], ["ps%d" % bB])
            if not fin:
                return
            act(rstd3[:, 0:512], ps[bA][:, 0:512], ACT.Ln, ["ps%d" % bA, "eps_c"], ["p3rstd"], bias=eps_c, scale=1.0 / 1024)
            act(rstd3[:, 512:576], ps[bB][:, 0:64], ACT.Ln, ["ps%d" % bB, "eps_c"], ["p3rstd"], bias=eps_c, scale=1.0 / 1024)
            act(rstd3, rstd3, ACT.Exp, ["p3rstd"], ["p3rstd"], scale=-0.5)

        def p3_apply(T, extra=()):
            for k in range(8):
                P.op("vector", lambda e_, o=xb3[:, k, :], a_=xs3[:, k, :], s_=g1[:, k:k + 1]: e_.scalar_tensor_tensor(
                    out=o, in0=a_, scalar=s_, in1=rstd3, op0=ALU.mult, op1=ALU.mult), ["p3xs", "p3rstd", "g1"], [("p3xb", k)], extra=extra)


        SBK = [[4, 5], [7, 6]]
        OBK = [0, 1, 2]

        NT_ATT = 0 if STOP < 4 else (1 if STOP < 5 else 4)
        allsteps = []
        pidx = 0
        for T in range(NT_ATT):
            steps = [(32, 16, 0, None)]
            for s_ in range(4 * T):
                steps.append((s_, 128, 0, None))
                steps.append((16 + s_, 128, 0, None))
            for i in range(4):
                steps.append((4 * T + i, 128, 128 * i, 0))
                steps.append((16 + 4 * T + i, 128, 128 * i, 1 + (i % 2)))
            for p in range(4):
                for si, st in enumerate(steps):
                    allsteps.append(dict(T=T, p=p, si=si, n=len(steps), st=st, pidx=pidx))
                pidx += 1

        def emit_scores(j):
            a = allsteps[j]
            T, p = a["T"], a["p"]
            kb, nk, q0, mi = a["st"]
            rp_, rk = Rp[T % 2], "Rp%d" % (T % 2)
            kc0 = META0 if kb == 32 else kb * 128
            kkey = ("KT", p, (kc0 // 512) * 512 if kb != 32 else META0)
            for e in range(2):
                sb = SBK[j % 2][e]
                mm(ps[sb][0:nk, q0:512], KT[e * 64:(e + 1) * 64, p, kc0:kc0 + nk], QT[e * 64:(e + 1) * 64, p, T * 512 + q0:(T + 1) * 512],
                   True, False, [kkey, ("QT", p, T)], ["ps%d" % sb])
            for e in range(2):
                sb = SBK[j % 2][e]
                mm(ps[sb][0:nk, q0:512], onesP[e * 64:(e + 1) * 64, 0:nk], rp_[e * 64:(e + 1) * 64, p, q0:512], False, mi is None,
                   [rk, "onesP"], ["ps%d" % sb])
            if mi is not None:
                for e in range(2):
                    sb = SBK[j % 2][e]
                    mm(ps[sb][0:nk, q0:q0 + 128], ident_bf, masks_bf[:, mi * 128:(mi + 1) * 128], False, True,
                       ["ident_bf", "masks"], ["ps%d" % sb])

        def obank(a, e):
            return OBK[(2 * a["pidx"] + e) % 3]

        def emit_exp_pv(j):
            a = allsteps[j]
            T, p = a["T"], a["p"]
            kb, nk, q0, mi = a["st"]
            for e in range(2):
                sb = SBK[j % 2][e]
                pt, ptk = PT[j % 2][e], "PT%d_%d" % (j % 2, e)
                act(pt[0:nk, q0:512], ps[sb][0:nk, q0:512], ACT.Exp, ["ps%d" % sb, "NC"], [ptk], bias=NC[0:nk, 2 * p + e, kb:kb + 1], scale=0.125)
            for e in range(2):
                pt, ptk = PT[j % 2][e], "PT%d_%d" % (j % 2, e)
                ob = obank(a, e)
                mm(ps[ob][:, q0:512], VP[0:nk, kb, p, e * 64:e * 64 + 128], pt[0:nk, q0:512], a["si"] == 0, a["si"] == a["n"] - 1,
                   [ptk, ("VP", kb, 0), ("VP", kb, 1), "VP"], ["ps%d" % ob])

        nrm = [0]

        def norm(a, final=False):
            T, p = a["T"], a["p"]
            for e in range(2):
                ob = obank(a, e)
                i = nrm[0] % 2
                nrm[0] += 1
                o0, d0 = (0, 64) if e == 0 else (64, 0)
                if final:
                    copy("vector", Dsb[i][o0:o0 + 64, :], ps[ob][d0:d0 + 64, :], ["ps%d" % ob], ["Dsb%d" % i])
                    copy("scalar", Osb[i][o0:o0 + 64, :], ps[ob][o0:o0 + 64, :], ["ps%d" % ob, "Dsb%d" % i], ["Osb%d" % i])
                    act(Dsb[i][o0:o0 + 64, :], Dsb[i][o0:o0 + 64, :], ACT.Ln, ["Dsb%d" % i], ["Dsb%d" % i])
                    act(Dsb[i][o0:o0 + 64, :], Dsb[i][o0:o0 + 64, :], ACT.Exp, ["Dsb%d" % i], ["Dsb%d" % i], scale=-1.0)
                    tt("gpsimd", OATT[o0:o0 + 64, p, T * 512:(T + 1) * 512], Osb[i][o0:o0 + 64, :], Dsb[i][o0:o0 + 64, :], ALU.mult,
                       ["Osb%d" % i, "Dsb%d" % i], [("OATT", p, T)])
                    continue
                copy("vector", Osb[i][o0:o0 + 64, :], ps[ob][o0:o0 + 64, :], ["ps%d" % ob], ["Osb%d" % i])
                copy("vector", Dsb[i][o0:o0 + 64, :], ps[ob][d0:d0 + 64, :], ["ps%d" % ob], ["Dsb%d" % i])
                P.op("vector", lambda e_, o=Dsb[i][o0:o0 + 64, :]: e_.reciprocal(out=o, in_=o), ["Dsb%d" % i], ["Dsb%d" % i])
                tt("gpsimd", OATT[o0:o0 + 64, p, T * 512:(T + 1) * 512], Osb[i][o0:o0 + 64, :], Dsb[i][o0:o0 + 64, :], ALU.mult,
                   ["Osb%d" % i, "Dsb%d" % i], [("OATT", p, T)])

        NS = len(allsteps)
        if NS:
            if not rgen0_done[0]:
                prep_attn_consts(ph1_fence)
                rgen(0)
            emit_scores(0)
        early = [None]
        fenceA = [None]
        for j in range(NS):
            a = allsteps[j]
            if NT_ATT == 4 and a["T"] == 3 and a["p"] == 3:
                if a["si"] == 0:
                    early[0] = P.fence_set()
                    p3_issue(0, extra=early[0])
                if a["si"] == 24:
                    p3_rstd(0, banks=(MB, OBK[2]), extra=early[0], on_dve=True)
            if j + 1 < NS:
                emit_scores(j + 1)
            emit_exp_pv(j)
            if a["p"] == 0 and a["si"] == 3 and a["T"] + 1 < NT_ATT and not (a["T"] == 0 and rgen0_done[0]):
                rgen(a["T"] + 1, part=1)
            offs = (0, 1, 2, 3) if a["n"] < 12 else (0, 2, 4, 6)
            if a["p"] == 2 and a["T"] + 1 < NT_ATT and (a["n"] - 1 - a["si"]) in offs:
                rgen(a["T"] + 1, part=2, pair=3 - offs.index(a["n"] - 1 - a["si"]))
            if a["si"] == a["n"] - 1:
                if j == NS - 1 and early[0] is not None and STOP >= 6:
                    fenceA[0] = P.fence_set()
                    p3_apply(0, extra=fenceA[0])
                    dma("gpsimd", PW, pool_w.rearrange("g c d -> c g d"), [], ["PW"], "w3_pw", extra=fenceA[0])
                    dma("gpsimd", WUA, w_up_attn.rearrange("(c p) n -> p c n", p=128), [], ["WUA"], "w3_ua", extra=fenceA[0])
                    dma("gpsimd", WUP, w_up_pool.rearrange("(c p) n -> p c n", p=128), [], ["WUP"], "w3_up", extra=fenceA[0])
                    dma("gpsimd", WO, w_out.rearrange("(k p) n -> p k n", p=128), [], ["WO"], "w3_wo", extra=fenceA[0])
                    norm(a, final=True)
                else:
                    norm(a)

        if DEBUG:
            dma("sync", dbg_kt, KT.rearrange("p c n -> p (c n)"), [("KT", p, c) for p in range(4) for c in [0, 512, 1024, 1536, 2048, 2560, 3072, 3584, META0]], [], "dbg")
            dma("sync", dbg_nc, NC.rearrange("p h b -> p (h b)"), ["NC"], [], "dbg")
            dma("sync", dbg_qt, QT.rearrange("p c n -> p (c n)"), [("QT", p, t) for p in range(4) for t in range(4)], [], "dbg")
            dma("sync", dbg_vp, VP.rearrange("p b c n -> p (b c n)"), ["VP"], [], "dbg")

        if STOP < 6:
            fin = P.fence_set()
            P.wait_only("sync", fin)
            P.emit(nc)
            return nc
        fence = P.fence_set()
        if fenceA[0] is not None:
            for e in ENGS:
                P.wait_only(e, fenceA[0])
        else:
            for e in ENGS:
                P.wait_only(e, fence)
        if DEBUG:
            dma("sync", dbg_oatt, OATT.rearrange("p c n -> p (c n)"), [], [], "dbg")
        if fenceA[0] is None:
            dma("gpsimd", PW, pool_w.rearrange("g c d -> c g d"), [], ["PW"], "w3_pw")
            dma("gpsimd", WUA, w_up_attn.rearrange("(c p) n -> p c n", p=128), [], ["WUA"], "w3_ua")
            dma("gpsimd", WUP, w_up_pool.rearrange("(c p) n -> p c n", p=128), [], ["WUP"], "w3_up")
            dma("gpsimd", WO, w_out.rearrange("(k p) n -> p k n", p=128), [], ["WO"], "w3_wo")

        def proj(col, n0, n):
            b = bank()
            for k in range(8):
                mm(ps[b][:, 0:n], W2[:, k, col:col + 128], xb3[:, k, n0:n0 + n], k == 0, k == 7, [("p3xb", k), w2key(col)], ["ps%d" % b])
            return b

        def gate2(b):
            i = thc[0] % 2
            thc[0] += 1
            act(TTt[i], ps[b][:, :], ACT.Silu, ["ps%d" % b], ["TT%d" % i])
            return TTt[i], "TT%d" % i

        if early[0] is None:
            p3_issue(0)
            p3_rstd(0)
        if fenceA[0] is None:
            p3_apply(0)
        bank_first.extend([4, 5, 6, 7, 2, 3])
        pb[0] = 0
        for T in range(4):
            if T + 1 < 4:
                p3_issue(T + 1)
            for blk in range(4):
                row0 = T * 512 + blk * 128
                dma("sync", RES[blk], x_own[row0:row0 + 128, :], [], [("RES", blk)], "res%d" % blk, extra=(fence if T == 0 else ()))
            def uproj(g):
                bm = proj(g * 128, 0, 512)
                bh = proj(g * 128, 512, 64)
                uk = "U%d" % g
                copy("scalar", U[g][:, :, 16:144], ps[bm][:, :].rearrange("p (b n) -> p b n", b=4), ["ps%d" % bm], [uk])
                copy("scalar", U[g][:, :, 0:16], ps[bh][:, 0:64].rearrange("p (b n) -> p b n", b=4), ["ps%d" % bh], [uk])

            def chain_pool(g):
                src, srck = U[g], "U%d" % g
                bufs = [(S1, "S1"), (S2, "S2")]
                w = 1
                lo = 0
                for lvl in range(g + 1):
                    dst, dstk = bufs[lvl % 2]
                    lo2 = lo + w
                    tt("vector" if g <= 1 else "gpsimd", dst[:, :, lo2:144], src[:, :, lo2:144], src[:, :, lo2 - w:144 - w], ALU.add, [srck], [dstk])
                    src, srck, lo, w = dst, dstk, lo2, 2 * w
                return src, srck, w

            def chain_fin(g, c3):
                src, srck, w = c3
                stt("vector", PL[g].rearrange("p (b n) -> p b n", b=4), U[g][:, :, 16:144], -float(w), src[:, :, 16:144], ALU.mult, ALU.add,
                    [srck, "U%d" % g], ["PL%d" % g])

            def zattn(c):
                bz = proj(1024 + c * 128, 0, 512)
                t2, t2k = gate2(bz)
                tt("vector", YA[:, c, :], OATT[:, c, T * 512:(T + 1) * 512], t2, ALU.mult, [t2k, ("OATT", c, T)], [("YA", c)])

            uproj(3)
            c3 = chain_pool(3)
            uproj(2)
            uproj(1)
            uproj(0)
            chain_fin(3, c3)
            c2 = chain_pool(2)
            zattn(0)
            zattn(1)
            chain_fin(2, c2)
            c1 = chain_pool(1)
            zattn(2)
            chain_fin(1, c1)
            c0 = chain_pool(0)
            zattn(3)
            chain_fin(0, c0)
            for g in (3, 2, 1, 0):
                bz = proj(512 + g * 128, 0, 512)
                t2, t2k = gate2(bz)
                by = bank()
                mm(ps[by][:, :], PW[:, g, :], PL[g], True, True, ["PL%d" % g, "PW"], ["ps%d" % by])
                stt("vector", YP[:, g, :], ps[by][:, :], hps[:, g:g + 1], t2, ALU.mult, ALU.mult, ["ps%d" % by, t2k, "hps"], [("YP", g)])
            for m in range(8):
                nxt = T + 1 < 4
                if nxt and m == 1:
                    rb = (bank(), bank())
                    reserved.update(rb)
                if nxt and 1 <= m <= 4:
                    p3_rstd(T + 1, banks=rb, ks=range(2 * (m - 1), 2 * m), do_mm=False, fin=False)
                bgp = proj(1536 + m * 128, 0, 512)
                bga = proj(2560 + m * 128, 0, 512)
                if nxt and 1 <= m <= 4:
                    p3_rstd(T + 1, banks=rb, ks=range(2 * (m - 1), 2 * m), do_sq=False, fin=False)
                bup = bank()
                for c in range(4):
                    mm(ps[bup][:, :], WUP[:, c, m * 128:(m + 1) * 128], YP[:, c, :], c == 0, c == 3, [("YP", c), "WUP"], ["ps%d" % bup])
                bua = bank()
                for c in range(4):
                    mm(ps[bua][:, :], WUA[:, c, m * 128:(m + 1) * 128], YA[:, c, :], c == 0, c == 3, [("YA", c), "WUA"], ["ps%d" % bua])
                i = thc[0] % 2
                thc[0] += 1
                act(TH[i], ps[bgp][:, :], ACT.Tanh, ["ps%d" % bgp], ["TH%d" % i], scale=0.5)
                stt("vector", M1, TH[i], 1.0, ps[bup][:, :], ALU.add, ALU.mult, ["TH%d" % i, "ps%d" % bup], ["M1"])
                i = thc[0] % 2
                thc[0] += 1
                act(TH[i], ps[bga][:, :], ACT.Tanh, ["ps%d" % bga], ["TH%d" % i], scale=0.5)
                stt("vector", M2, TH[i], 1.0, ps[bua][:, :], ALU.add, ALU.mult, ["TH%d" % i, "ps%d" % bua], ["M2"])
                tt("vector" if m == 7 else "gpsimd", MG[:, m, :], M1, M2, ALU.add, ["M1", "M2"], [("MG", m)])
                if m == 5 and nxt:
                    p3_rstd(T + 1, banks=rb, ks=(), fin=True)
                    reserved.clear()
            if T + 1 < 4:
                p3_apply(T + 1)
            memset("vector", ss4, 0.0, ["ss4"])
            for blk in range(4):
                row0 = T * 512 + blk * 128
                sk = ("ss4", blk)
                for hf in range(2):
                    bo = bank()
                    for m in range(8):
                        mm(ps[bo][:, :], MG[:, m, blk * 128:(blk + 1) * 128], WO[:, m, hf * 512:(hf + 1) * 512], m == 0, m == 7,
                           [("MG", m), "WO"], ["ps%d" % bo])
                    stt("vector", RES[blk][:, hf * 512:(hf + 1) * 512], ps[bo][:, :], 0.5, RES[blk][:, hf * 512:(hf + 1) * 512], ALU.mult, ALU.add,
                        ["ps%d" % bo, ("RES", blk)], [("RES", blk)])
                act(PLj[:, 0:1024], RES[blk], ACT.Square, [("RES", blk), "ss4"], ["PL0", "PL1", sk], accum=ss4[:, blk:blk + 1])
                act(ss4[:, blk:blk + 1], ss4[:, blk:blk + 1], ACT.Ln, [sk, "eps_c"], [sk], bias=eps_c, scale=1.0 / 1024)
                act(ss4[:, blk:blk + 1], ss4[:, blk:blk + 1], ACT.Exp, [sk], [sk], scale=-0.5)
                for pb_ in ([blk - 1] if blk > 0 else []) + ([blk] if blk == 3 else []):
                    r0_ = T * 512 + pb_ * 128
                    stt("vector", RES[pb_], RES[pb_], ss4[:, pb_:pb_ + 1], fng_b, ALU.mult, ALU.mult, [("RES", pb_), ("ss4", pb_), "fng"], [("RES", pb_)])
                    dma("sync", out[r0_:r0_ + 128, :], RES[pb_], [("RES", pb_)], [], "out%d" % pb_)

        fin = P.fence_set()
        P.wait_only("sync", fin)
        P.emit(nc)
    return nc


def _core_layout(j):
    a = [2 * s + ((s % 2) if j == 0 else 1 - (s % 2)) for s in range(16)]
    b = [2 * s + (1 - (s % 2) if j == 0 else (s % 2)) for s in range(16)]
    return a, b


def make_in_maps(x, meta_tokens, norm_g, w_in, b_forget, pool_w, pool_scale, w_up_pool, w_up_attn, w_out, final_norm_g):
    f = lambda v: np.ascontiguousarray(np.asarray(v, dtype=np.float32))
    x = f(x)
    meta = f(meta_tokens)
    shared = {
        "w_in": f(w_in)[0], "pool_w": f(pool_w)[0], "w_up_pool": f(w_up_pool)[0], "w_up_attn": f(w_up_attn)[0],
        "w_out": f(w_out)[0],
        "ng": np.ascontiguousarray(f(norm_g)[0].reshape(8, 128).T),
        "pscale": np.ascontiguousarray(f(pool_scale)[0].reshape(4, 128).T),
        "bfg": f(b_forget)[0], "fng": f(final_norm_g),
    }
    kk = np.arange(128)[:, None]
    qq = np.arange(128)[None, :]
    tri = np.where(kk <= qq, 0.0, NEG).astype(np.float32)
    in_maps = []
    for core in range(8):
        bidx, j = core // 2, core % 2
        a, b = _core_layout(j)
        seq = np.concatenate([meta, x[bidx]], 0)
        parts = [seq[16 + g * 128:16 + (g + 1) * 128] for g in a]
        parts += [seq[16 + g * 128:16 + (g + 1) * 128] for g in b]
        parts.append(seq[0:16])
        parts += [seq[g * 128:g * 128 + 16] for g in a]
        xloc = np.concatenate(parts, 0)
        gpos = np.array([1 + g for g in a] + [1 + g for g in b] + [0])
        pmat = (gpos[:, None] < gpos[None, :]).astype(np.float32)
        full = np.full((128, 128), NEG, np.float32)
        zero = np.zeros((128, 128), np.float32)
        mk = np.concatenate([tri, full if j == 0 else zero, zero if j == 0 else full], 1)
        m = dict(shared)
        m["xT"] = np.ascontiguousarray(xloc.T)
        m["x_own"] = np.ascontiguousarray(xloc[0:2048])
        m["pm"] = np.ascontiguousarray(pmat)
        m["masks"] = np.ascontiguousarray(mk)
        in_maps.append(m)
    return in_maps


def kernel(x, meta_tokens, norm_g, w_in, b_forget, pool_w, pool_scale, w_up_pool, w_up_attn, w_out, final_norm_g):
    in_maps = make_in_maps(x, meta_tokens, norm_g, w_in, b_forget, pool_w, pool_scale, w_up_pool, w_up_attn, w_out, final_norm_g)
    nc = build_nc()
    res = run_bass_kernel_spmd(nc, in_maps, core_ids=list(range(8)))
    outp = np.zeros((4, 4096, 1024), np.float32)
    for core in range(8):
        bidx, j = core // 2, core % 2
        a, _ = _core_layout(j)
        o = np.asarray(res.results[core]["out"], dtype=np.float32)
        for s, g in enumerate(a):
            outp[bidx, g * 128:(g + 1) * 128] = o[s * 128:(s + 1) * 128]
    return outp
```

```python
from contextlib import ExitStack
import numpy as np
import concourse.bass as bass
import concourse.mybir as mybir
from concourse.bass_utils import run_bass_kernel_spmd

F32 = mybir.dt.float32
BF16 = mybir.dt.bfloat16
ALU = mybir.AluOpType
ACT = mybir.ActivationFunctionType

import os
DEBUG = False
STOP = int(os.environ.get("KSTOP", "99"))
SUB = int(os.environ.get("KSUB", "99"))
NEG = -240000.0
EPS = 1e-6
NTOK = 4368
META0 = 4096
HALO0 = 4112
ENGS = ["tensor", "vector", "scalar", "gpsimd", "sync"]
EPOCH = 1000


class Prog:
    def __init__(self):
        self.ops = {e: [] for e in ENGS}
        self.last_writer = {}
        self.readers = {}
        self.all_ops = []
        self.dma_counts = {}

    def op(self, eng, fn, reads=(), writes=(), dma=None, extra=()):
        deps = list(extra)
        for b in reads:
            w = self.last_writer.get(b)
            if w is not None:
                deps.append(w)
        for b in writes:
            w = self.last_writer.get(b)
            if w is not None:
                deps.append(w)
            deps.extend(self.readers.get(b, ()))
        rec = dict(eng=eng, fn=fn, deps=[], dma=dma, sig=False, id=len(self.all_ops))
        seen = set()
        raw = set(id(self.last_writer.get(b)) for b in reads)
        for d in deps:
            if d["id"] in seen:
                continue
            seen.add(d["id"])
            if d["eng"] == eng and d["dma"] is None and dma is None and eng != "gpsimd":
                if id(d) not in raw:
                    continue
            rec["deps"].append(d)
            d["sig"] = True
        if dma is not None:
            self.dma_counts[dma] = self.dma_counts.get(dma, 0) + 1
            rec["dma_n"] = self.dma_counts[dma]
        for b in reads:
            self.readers.setdefault(b, []).append(rec)
        for b in writes:
            self.last_writer[b] = rec
            self.readers[b] = []
        self.ops[eng].append(rec)
        self.all_ops.append(rec)
        return rec

    def wait_only(self, eng, recs):
        rec = dict(eng=eng, fn=None, deps=list(recs), dma=None, sig=False, id=len(self.all_ops))
        for d in recs:
            d["sig"] = True
        self.ops[eng].append(rec)
        self.all_ops.append(rec)
        return rec

    def fence_set(self):
        recs = []
        for e in ENGS:
            for r in reversed(self.ops[e]):
                if r["fn"] is not None and r["dma"] is None:
                    recs.append(r)
                    break
        last = {}
        for r in self.all_ops:
            if r["dma"] is not None:
                last[r["dma"]] = r
        recs.extend(last.values())
        return recs

    def emit(self, nc):
        nsem = {}
        for e in ENGS:
            n = 0
            for r in self.ops[e]:
                if r["dma"] is None and r["sig"] and r["fn"] is not None:
                    n += 1
                    r["sig_n"] = n
            nsem[e] = (n + EPOCH - 1) // EPOCH
        with ExitStack() as es:
            esem = {e: [es.enter_context(nc.semaphore(f"s_{e}_{i}")) for i in range(nsem[e])] for e in ENGS}
            dsem = {k: es.enter_context(nc.semaphore(f"d_{k}")) for k in self.dma_counts}
            block = es.enter_context(nc.Block())

            def make(e):
                def body(engine):
                    waited = {}
                    for r in self.ops[e]:
                        for d in r["deps"]:
                            if d["dma"] is not None:
                                sem, val, key = dsem[d["dma"]], 16 * d["dma_n"], ("d", d["dma"])
                            else:
                                ep, v = divmod(d["sig_n"] - 1, EPOCH)
                                sem, val, key = esem[d["eng"]][ep], v + 1, (d["eng"], ep)
                            if waited.get(key, 0) >= val:
                                continue
                            waited[key] = val
                            engine.wait_ge(sem, val)
                        if r["fn"] is None:
                            continue
                        inst = r["fn"](engine)
                        if r["dma"] is not None:
                            inst.then_inc(dsem[r["dma"]], 16)
                        elif r["sig"]:
                            ep, v = divmod(r["sig_n"] - 1, EPOCH)
                            inst.then_inc(esem[e][ep], 1)
                return body

            for e in ENGS:
                if self.ops[e]:
                    getattr(block, e)(make(e))


def build_nc():
    nc = bass.Bass("TRN2", target_bir_lowering=False)
    D = lambda name, shape, kind="ExternalInput": nc.dram_tensor(name, shape, F32, kind=kind).ap()
    xT = D("xT", [1024, NTOK])
    x_own = D("x_own", [2048, 1024])
    w_in = D("w_in", [1024, 5128])
    pool_w = D("pool_w", [4, 128, 128])
    w_up_pool = D("w_up_pool", [512, 1024])
    w_up_attn = D("w_up_attn", [512, 1024])
    w_out = D("w_out", [1024, 1024])
    ng = D("ng", [128, 8])
    pscale = D("pscale", [128, 4])
    bfg = D("bfg", [8])
    fng = D("fng", [1024])
    pm = D("pm", [33, 33])
    masks = D("masks", [128, 384])
    out = D("out", [2048, 1024], kind="ExternalOutput")
    if DEBUG:
        dbg_kt = D("dbg_kt", [128, 4 * 4112], kind="ExternalOutput")
        dbg_nc = D("dbg_nc", [128, 264], kind="ExternalOutput")
        dbg_oatt = D("dbg_oatt", [128, 4 * 2048], kind="ExternalOutput")
        dbg_qt = D("dbg_qt", [128, 4 * 2048], kind="ExternalOutput")
        dbg_vp = D("dbg_vp", [128, 33 * 768], kind="ExternalOutput")

    xT_v = xT.rearrange("(k p) n -> p k n", p=128)
    w_in_v = w_in.rearrange("(k p) n -> p k n", p=128)

    P = Prog()
    with ExitStack() as es:
        ARENA = 106000
        arena = es.enter_context(nc.sbuf_tensor("arena", [128, ARENA], BF16))
        ps = [es.enter_context(nc.psum_tensor(f"ps{i}", [128, 512], F32)) for i in range(8)]

        def view(off, nfree, dt, parts=128):
            assert off % 4 == 0
            nb = nfree * (4 if dt == F32 else 2)
            a = arena[0:parts, off // 2:(off + nb) // 2]
            if dt == F32:
                a = a.bitcast(F32)
            return a

        class Bump:
            def __init__(self, start, end):
                self.o, self.end = start, end

            def __call__(self, nfree, dt, parts=128):
                nb = nfree * (4 if dt == F32 else 2)
                nb = (nb + 63) // 64 * 64
                v = view(self.o, nfree, dt, parts)
                self.o += nb
                assert self.o <= self.end, (self.o, self.end)
                return v

        C0, C1 = 0, 8192
        O0, O1 = C1, C1 + 16384
        T0, T1 = O1, O1 + 57344
        A0, A1 = T1, T1 + 101056
        D0, D1 = A1, ARENA * 2
        cb = Bump(C0, C1)
        ident_bf = cb(128, BF16)
        ones_bf = cb(128, BF16)
        ones_f = cb(128, F32)
        tri_f = cb(128, F32)
        masks_bf = cb(384, BF16)
        g1 = cb(8, F32)
        hps = cb(4, F32)
        bfb = cb(8, F32)
        pm_sb = cb(33, F32, parts=33)
        fng_b = cb(1024, F32)
        eps_c = cb(1, F32)
        act_warm = cb(1, F32)
        ident_f = cb(128, F32)
        OATT = view(O0, 4 * 2048, BF16).rearrange("p (c n) -> p c n", c=4)

        ab = Bump(A0, A1)
        KT = ab(4 * 4112, BF16).rearrange("p (c n) -> p c n", c=4)
        QT = ab(4 * 2048, BF16).rearrange("p (c n) -> p c n", c=4)
        VP = ab(33 * 768, BF16).rearrange("p (b c n) -> p b c n", b=33, c=4)
        NC = ab(264, F32).rearrange("p (h b) -> p h b", h=8)

        tb = Bump(T0, T1)
        W1q = tb(8 * 512, BF16).rearrange("p (k n) -> p k n", k=8)
        W1k = tb(8 * 512, BF16).rearrange("p (k n) -> p k n", k=8)
        W1v = tb(8 * 512, BF16).rearrange("p (k n) -> p k n", k=8)
        W1f = tb(8 * 8, BF16).rearrange("p (k n) -> p k n", k=8)
        xs = tb(8 * 512, F32).rearrange("p (k n) -> p k n", k=8)
        xb = tb(8 * 512, BF16).rearrange("p (k n) -> p k n", k=8)
        rstd_b = tb(528, F32)
        NLF = tb(264, F32).rearrange("p (h b) -> p h b", h=8)
        ft0 = tb(32, F32).rearrange("p (b h) -> p b h", b=4)
        ft1 = tb(32, F32).rearrange("p (b h) -> p b h", b=4)
        ST = tb(8, F32, parts=33)
        STrep = tb(8 * 128, F32, parts=33).rearrange("p (h n) -> p h n", h=8)

        db = Bump(D0, D1)
        PT = [[db(512, BF16) for _ in range(2)] for _ in range(2)]
        Osb = [db(512, F32) for _ in range(2)]
        Dsb = [db(512, F32) for _ in range(2)]
        negones = db(512, F32)
        Rp = [db(4 * 512, BF16).rearrange("p (c n) -> p c n", c=4) for _ in range(2)]
        ZR = [db(16 * 65, BF16).rearrange("p (s c n) -> p s c n", s=4, c=4) for _ in range(2)]
        onesP = db(128, BF16)

        def mm(o, lhsT, rhs, start, stop, reads, writes, extra=()):
            return P.op("tensor", lambda e: e.matmul(o, lhsT=lhsT, rhs=rhs, start=start, stop=stop), reads, writes, extra=extra)

        def act(o, i, func, reads, writes, bias=None, scale=None, accum=None):
            kw = {}
            if bias is not None:
                kw["bias"] = bias
            if scale is not None:
                kw["scale"] = scale
            if accum is not None:
                kw["accum_out"] = accum
            return P.op("scalar", lambda e: e.activation(out=o, in_=i, func=func, **kw), reads, writes)

        def copy(eng, o, i, reads, writes):
            if eng == "scalar":
                return P.op("scalar", lambda e: e.activation(out=o, in_=i, func=ACT.Copy), reads, writes)
            return P.op(eng, lambda e: e.tensor_copy(out=o, in_=i), reads, writes)

        def tt(eng, o, a, b, op, reads, writes):
            return P.op(eng, lambda e: e.tensor_tensor(out=o, in0=a, in1=b, op=op), reads, writes)

        def stt(eng, o, a, s, b, op0, op1, reads, writes):
            return P.op(eng, lambda e: e.scalar_tensor_tensor(out=o, in0=a, scalar=s, in1=b, op0=op0, op1=op1), reads, writes)

        def ts(eng, o, a, s1, s2, op0, op1, reads, writes):
            if s2 is None:
                return P.op(eng, lambda e: e.tensor_scalar(out=o, in0=a, scalar1=s1, scalar2=None, op0=op0), reads, writes)
            return P.op(eng, lambda e: e.tensor_scalar(out=o, in0=a, scalar1=s1, scalar2=s2, op0=op0, op1=op1), reads, writes)

        def memset(eng, o, val, writes):
            return P.op(eng, lambda e: e.memset(o, val), (), writes)

        def dma(q, o, i, reads, writes, key, extra=()):
            return P.op(q, lambda e: e.dma_start(out=o, in_=i), reads, writes, dma=key, extra=extra)

        dma("sync", g1, ng, [], ["g1"], "c_g1")
        dma("sync", bfb, bfg.partition_broadcast(128), [], ["bfb"], "c_bfb")
        memset("vector", ones_bf, 1.0, ["ones_bf"])
        memset("vector", ones_f, 1.0, ["ones_f"])
        memset("vector", eps_c, EPS, ["eps_c"])
        act(act_warm, eps_c, ACT.Ln, ["eps_c"], ["act_warm"])
        memset("vector", NLF, 0.0, ["NLF"])
        memset("gpsimd", tri_f, 1.0, ["tri_f"])
        P.op("gpsimd", lambda e: e.affine_select(out=tri_f, in_=tri_f, pattern=[[1, 128]], compare_op=ALU.is_ge,
                                                 fill=0.0, base=0, channel_multiplier=-1), ["tri_f"], ["tri_f"])
        memset("gpsimd", ident_f, 1.0, ["ident_f"])
        P.op("gpsimd", lambda e: e.affine_select(out=ident_f, in_=ident_f, pattern=[[1, 128]], compare_op=ALU.is_ge,
                                                 fill=0.0, base=0, channel_multiplier=-1), ["ident_f"], ["ident_f"])
        P.op("gpsimd", lambda e: e.affine_select(out=ident_f, in_=ident_f, pattern=[[-1, 128]], compare_op=ALU.is_ge,
                                                 fill=0.0, base=0, channel_multiplier=1), ["ident_f"], ["ident_f"])

        pb = [0]

        bank_first = []

        reserved = set()

        def bank():
            if bank_first:
                return bank_first.pop(0)
            while True:
                i = pb[0]
                pb[0] = (pb[0] + 1) % 8
                if i not in reserved:
                    return i

        def issue_x(xs_t, col0, n, xsk, dkey, sub=(), extra=()):
            r = dma("sync", xs_t[:, :, 0:n], xT_v[:, :, col0:col0 + n], [], [xsk], dkey, extra=extra)
            for (c0, cn, off) in sub:
                r = dma("sync", xs_t[:, :, off:off + cn], xT_v[:, :, c0:c0 + cn], [], [xsk], dkey, extra=extra)
            return r

        def norm_sq(xs_t, xb_t, ntot, xsk, xbk):
            for h_ in range(2):
                act(xb_t[:, 4 * h_:4 * h_ + 4, 0:ntot], xs_t[:, 4 * h_:4 * h_ + 4, 0:ntot], ACT.Square, [xsk],
                    [(xbk, k_) for k_ in range(4 * h_, 4 * h_ + 4)])

        def norm_x(xs_t, xb_t, rstd_t, ntot, xsk, xbk, rk, sq=True):
            if sq:
                norm_sq(xs_t, xb_t, ntot, xsk, xbk)
            done = 0
            while done < ntot:
                cn = min(512, ntot - done)
                b = bank()
                for k in range(8):
                    mm(ps[b][:, 0:cn], ones_bf, xb_t[:, k, done:done + cn], k == 0, k == 7, [(xbk, k), "ones_bf"], ["ps%d" % b])
                act(rstd_t[:, done:done + cn], ps[b][:, 0:cn], ACT.Ln, ["ps%d" % b, "eps_c"], [rk], bias=eps_c, scale=1.0 / 1024)
                done += cn
            act(rstd_t[:, 0:ntot], rstd_t[:, 0:ntot], ACT.Exp, [rk], [rk], scale=-0.5)
            for k in range(8):
                stt("vector", xb_t[:, k, 0:ntot], xs_t[:, k, 0:ntot], g1[:, k:k + 1], rstd_t[:, 0:ntot],
                    ALU.mult, ALU.mult, [xsk, rk, "g1"], [(xbk, k)])

        def load_norm(xs_t, xb_t, rstd_t, col0, n, tag, sub=()):
            issue_x(xs_t, col0, n, tag + "xs", tag + "x", sub)
            norm_x(xs_t, xb_t, rstd_t, n + sum(s_[1] for s_ in sub), tag + "xs", tag + "xb", tag + "rstd")

        tiles = []
        for t in range(4):
            tiles.append((2048 + t * 512, 512, 16 + 4 * t, False, -1))
        for t in range(4):
            tiles.append((t * 512, 512, 4 * t, True, t))
        ev = [0]

        def evac_eng():
            ev[0] += 1
            return "scalar" if ev[0] % 2 == 0 else "vector"

        if STOP < 1:
            tiles = []
        elif STOP < 2:
            tiles = tiles[0:1]
        xs2 = view(D0, 8 * 528, F32).rearrange("p (k n) -> p k n", k=8)
        xb2 = view(D0 + 16896, 8 * 528, BF16).rearrange("p (k n) -> p k n", k=8)
        XS = [xs2, xs]
        XB = [xb2, xb]
        NTL = len(tiles)

        def p1_n(ti):
            return tiles[ti][1] + (16 if ti == 0 else 0)

        def p1_issue(ti, extra=()):
            return issue_x(XS[ti % 2], tiles[ti][0], tiles[ti][1], "p1xs%d" % (ti % 2), "p1x%d" % (ti % 2),
                           sub=([(META0, 16, 512)] if ti == 0 else ()), extra=extra)

        def p1_sq(ti):
            norm_sq(XS[ti % 2], XB[ti % 2], p1_n(ti), "p1xs%d" % (ti % 2), "p1xb%d" % (ti % 2))

        def p1_norm(ti, sq=True):
            norm_x(XS[ti % 2], XB[ti % 2], rstd_b, p1_n(ti), "p1xs%d" % (ti % 2), "p1xb%d" % (ti % 2), "p1rstd", sq=sq)

        cs_bw = [None]

        def emit_cumsum(part):
            if part == 1:
                bw = cs_bw[0]
                for h in range(8):
                    mm(ps[bw][:, h * 33:(h + 1) * 33], STrep[:, h, :], pm_sb, False, h == 7, ["STrep", "pm"], ["ps%d" % bw])
                copy("vector", NC.rearrange("p h b -> p (h b)"), ps[bw][:, 0:264], ["ps%d" % bw], ["NC"])
                return
            bw = bank()
            cs_bw[0] = bw
            NLF2 = NLF.rearrange("p h b -> p (h b)")
            mm(ps[bw][:, 0:264], tri_f, NLF2, True, False, ["NLF", "tri_f"], ["ps%d" % bw])
            bs = bank()
            for h in range(8):
                mm(ps[bs][0:33, h:h + 1], NLF[:, h, :], ones_f[:, 0:1], True, True, ["NLF", "ones_f"], ["ps%d" % bs])
            copy("vector", ST, ps[bs][0:33, 0:8], ["ps%d" % bs], ["ST"])
            copy("vector", STrep, ST.unsqueeze(2).to_broadcast([33, 8, 128]), ["ST"], ["STrep"])

        MB = 3

        def rgen(T, part=0, pair=None):
            zr = ZR[T % 2]
            rp_ = Rp[T % 2]
            zk = "ZR%d" % (T % 2)
            rk = "Rp%d" % (T % 2)
            if part in (0, 1):
                ts("vector", zr[:, :, :, 0:65:64], NC[:, :, 4 * T:4 * T + 4].rearrange("p (c e) s -> p s c e", c=4), -8.0, None,
                   ALU.mult, ALU.bypass, ["NC"], [zk])
            if part == 1:
                return
            for p in (range(4) if pair is None else [pair]):
                for s_ in range(4):
                    mm(ps[MB][0:65, s_ * 128:(s_ + 1) * 128], zr[:, s_, p, :], ident_bf, True, True, [zk, "ident_bf"], ["ps%d" % MB])
                copy("vector", rp_[0:65, p, :], ps[MB][0:65, :], ["ps%d" % MB], [rk])

        def prep_attn_consts(extra):
            for i_, z in enumerate(ZR):
                P.op("gpsimd", lambda e_, z=z: e_.memset(z, 0.0), (), ["ZR%d" % i_], extra=extra)
            for i_, z in enumerate(Rp):
                P.op("gpsimd", lambda e_, z=z: e_.memset(z, 0.0), (), ["Rp%d" % i_], extra=extra)
            P.op("gpsimd", lambda e_: e_.memset(onesP, 0.0), (), ["onesP"], extra=extra)
            memset("gpsimd", negones, -1.0, ["negones"])
            memset("gpsimd", onesP[0:1, :], 1.0, ["onesP"])
            memset("gpsimd", onesP[64:65, :], 1.0, ["onesP"])

        r_x0 = p1_issue(0)
        r_wf = dma("gpsimd", W1f, w_in_v[:, :, 3072:3080], [], ["W1f"], "w1f", extra=[r_x0])
        r_wk = dma("gpsimd", W1k, w_in_v[:, :, 1536:2048], [], ["W1k"], "w1k", extra=[r_x0])
        memset("gpsimd", VP[:, :, :, 64:128], 1.0, ["VP"])
        dma("gpsimd", W1v, w_in_v[:, :, 2048:2560], [], ["W1v"], "w1v", extra=[r_wk])
        dma("sync", hps, pscale, [], ["hps"], "c_hps")
        dma("sync", pm_sb, pm, [], ["pm"], "c_pm")
        dma("sync", fng_b, fng.partition_broadcast(128), [], ["fng"], "c_fng")
        if NTL > 1:
            p1_issue(1, extra=[r_wk])
        dma("gpsimd", W1q, w_in_v[:, :, 1024:1536], [], ["W1q"], "w1q", extra=[r_wk])
        dma("gpsimd", masks_bf, masks, [], ["masks"], "c_masks")
        if NTL:
            p1_norm(0)
        fence7 = [None]
        rgen0_done = [False]

        def tile_parts(xbt, xbk, coff, n, lb0, col0, ot):
            nblk = (n + 127) // 128

            def f_path():
                bf_ = bank()
                for i in range(nblk):
                    m = min(128, n - i * 128)
                    for k in range(8):
                        mm(ps[bf_][0:m, i * 8:(i + 1) * 8], xbt[:, k, coff + i * 128:coff + i * 128 + m], W1f[:, k, :], k == 0, k == 7,
                           [(xbk, k), "W1f"], ["ps%d" % bf_])
                m = min(128, n)
                psf = ps[bf_][0:m, 0:nblk * 8].rearrange("p (b h) -> p b h", h=8)
                tt("vector", ft0[0:m, 0:nblk, :], psf, bfb[0:m, :].unsqueeze(1).to_broadcast([m, nblk, 8]), ALU.add,
                   ["ps%d" % bf_, "bfb"], ["ft0"])
                act(ft1[0:m, 0:nblk, :], ft0[0:m, 0:nblk, :], ACT.Exp, ["ft0"], ["ft1"], scale=-1.0)
                act(NLF[0:m, :, lb0:lb0 + nblk].rearrange("p h b -> p b h"), ft1[0:m, 0:nblk, :], ACT.Ln, ["ft1"], ["NLF"], bias=1.0)

            def k_proj():
                for p in range(4):
                    b = bank()
                    for k in range(8):
                        mm(ps[b][:, 0:n], W1k[:, k, p * 128:(p + 1) * 128], xbt[:, k, coff:coff + n], k == 0, k == 7, [(xbk, k), "W1k"], ["ps%d" % b])
                    copy(evac_eng(), KT[:, p, col0:col0 + n], ps[b][:, 0:n], ["ps%d" % b], [("KT", p, col0)])

            def v_proj():
                for i in range(nblk):
                    m = min(128, n - i * 128)
                    b = bank()
                    for k in range(8):
                        mm(ps[b][0:m, :], xbt[:, k, coff + i * 128:coff + i * 128 + m], W1v[:, k, :], k == 0, k == 7, [(xbk, k), "W1v"], ["ps%d" % b])
                    pv = ps[b][0:m, :].rearrange("p (c n) -> p c n", c=4)
                    ve = evac_eng()
                    copy(ve, VP[0:m, lb0 + i, :, 0:64], pv[:, :, 0:64], ["ps%d" % b, "VP"], [("VP", lb0 + i, 0)])
                    copy(ve, VP[0:m, lb0 + i, :, 128:192], pv[:, :, 64:128], ["ps%d" % b, "VP"], [("VP", lb0 + i, 1)])

            def q_proj():
                for p in range(4):
                    b = bank()
                    for k in range(8):
                        mm(ps[b][:, 0:n], W1q[:, k, p * 128:(p + 1) * 128], xbt[:, k, coff:coff + n], k == 0, k == 7, [(xbk, k), "W1q"], ["ps%d" % b])
                    copy(evac_eng(), QT[:, p, col0:col0 + n], ps[b][:, 0:n], ["ps%d" % b], [("QT", p, ot)])

            return f_path, k_proj, v_proj, q_proj

        for ti, (col0, n, lb0, own, ot) in enumerate(tiles):
            if ti == NTL - 1 and NTL == 8 and STOP >= 4:
                fence7[0] = P.fence_set()
                prep_attn_consts(fence7[0])
            xbt = XB[ti % 2]
            xbk = "p1xb%d" % (ti % 2)
            if ti + 1 < NTL:
                p1_sq(ti + 1)
            f_path, k_proj, v_proj, q_proj = tile_parts(xbt, xbk, 0, n, lb0, col0, ot)
            f_path()
            k_proj()
            if ti + 1 < NTL:
                p1_norm(ti + 1, sq=False)
            if ti + 2 < NTL:
                p1_issue(ti + 2)
            if ti == 0:
                mf, mk, mv, _ = tile_parts(xbt, xbk, 512, 16, 32, META0, -1)
                mf()
                mk()
            if ti == NTL - 1 and STOP >= 3:
                emit_cumsum(0)
            v_proj()
            if ti == 0:
                mv()
            if ti == NTL - 1 and STOP >= 3:
                emit_cumsum(1)
                if STOP >= 4 and fence7[0] is not None:
                    rgen(0)
                    rgen(1, part=1)
                    rgen0_done[0] = True
            if own:
                q_proj()
            if ti == 0:
                copy("vector", ident_bf, ident_f, ["ident_f"], ["ident_bf"])
                for g_ in range(4):
                    ts("vector", hps[:, g_:g_ + 1], hps[:, g_:g_ + 1], 1.0 / (2 ** (g_ + 1)), None, ALU.mult, ALU.bypass, ["hps"], ["hps"])

        if STOP < 3:
            fin = P.fence_set()
            P.wait_only("sync", fin)
            P.emit(nc)
            return nc
        ph1_fence = P.fence_set()

        W2 = view(T0, 8 * 3584, BF16).rearrange("p (k n) -> p k n", k=8)
        w2_srcs = [(0, 1024, 0), (2560, 512, 1024), (3080, 1024, 1536), (4104, 1024, 2560)]
        for wi, (c0, cn, d0) in enumerate(w2_srcs):
            dma("gpsimd", W2[:, :, d0:d0 + cn], w_in_v[:, :, c0:c0 + cn], [], [("W2", wi)], "w2_%d" % wi, extra=ph1_fence)

        def w2key(col):
            for wi, (c0, cn, d0) in enumerate(w2_srcs):
                if d0 <= col < d0 + cn:
                    return ("W2", wi)

        p3 = Bump(A0, D1)
        xs3 = p3(8 * 576, F32).rearrange("p (k n) -> p k n", k=8)
        sqk = [p3(576, BF16) for _ in range(2)]
        rstd3 = p3(576, F32)
        assert p3.o <= A0 + 3 * 8224
        WUP = p3(4 * 1024, BF16).rearrange("p (c n) -> p c n", c=4)
        WUA = p3(4 * 1024, BF16).rearrange("p (c n) -> p c n", c=4)
        WO = p3(8 * 1024, BF16).rearrange("p (k n) -> p k n", k=8)
        PW = p3(4 * 128, BF16).rearrange("p (g n) -> p g n", g=4)
        xb3 = p3(8 * 576, BF16).rearrange("p (k n) -> p k n", k=8)
        U = [p3(4 * 144, F32).rearrange("p (b n) -> p b n", b=4) for _ in range(4)]
        S1 = p3(4 * 144, F32).rearrange("p (b n) -> p b n", b=4)
        S2 = p3(4 * 144, F32).rearrange("p (b n) -> p b n", b=4)
        PLj = p3(2048, BF16)
        PL = [PLj[:, i_ * 512:(i_ + 1) * 512] for i_ in range(4)]
        YP = p3(4 * 512, BF16).rearrange("p (c n) -> p c n", c=4)
        YA = p3(4 * 512, BF16).rearrange("p (c n) -> p c n", c=4)
        TH = [p3(512, F32) for _ in range(2)]
        TTt = [p3(512, F32) for _ in range(2)]
        M1 = p3(512, F32)
        M2 = p3(512, F32)
        MG = p3(8 * 512, BF16).rearrange("p (k n) -> p k n", k=8)
        RES = [p3(1024, F32) for _ in range(4)]
        ss4 = p3(4, F32)

        thc = [0]

        def p3_issue(T, extra=()):
            issue_x(xs3, T * 512, 512, "p3xs", "p3x", sub=[(HALO0 + 64 * T, 64, 512)], extra=extra)

        def p3_rstd(T, banks=None, extra=(), on_dve=False, ks=range(8), do_sq=True, do_mm=True, fin=True):
            bA, bB = banks if banks is not None else (bank(), bank())
            for k in ks:
                sq = sqk[k % 2]
                if not do_sq:
                    pass
                elif on_dve:
                    P.op("vector", lambda e_, o=sq, i=xs3[:, k, :]: e_.tensor_tensor(out=o, in0=i, in1=i, op=ALU.mult), ["p3xs"], ["sqk%d" % (k % 2)], extra=extra)
                else:
                    P.op("scalar", lambda e_, o=sq, i=xs3[:, k, :]: e_.activation(out=o, in_=i, func=ACT.Square), ["p3xs"], ["sqk%d" % (k % 2)], extra=extra)
                if do_mm:
                    mm(ps[bA][:, 0:512], ones_bf, sq[:, 0:512], k == 0, k == 7, ["sqk%d" % (k % 2), "ones_bf"], ["ps%d" % bA])
                    mm(ps[bB][:, 0:64], ones_bf, sq[:, 512:576], k == 0, k == 7, ["sqk%d" % (k % 2), "ones_bf"], ["ps%d" % bB])
            if not fin:
                return
            act(rstd3[:, 0:512], ps[bA][:, 0:512], ACT.Ln, ["ps%d" % bA, "eps_c"], ["p3rstd"], bias=eps_c, scale=1.0 / 1024)
            act(rstd3[:, 512:576], ps[bB][:, 0:64], ACT.Ln, ["ps%d" % bB, "eps_c"], ["p3rstd"], bias=eps_c, scale=1.0 / 1024)
            act(rstd3, rstd3, ACT.Exp, ["p3rstd"], ["p3rstd"], scale=-0.5)

        def p3_apply(T, extra=()):
            for k in range(8):
                P.op("vector", lambda e_, o=xb3[:, k, :], a_=xs3[:, k, :], s_=g1[:, k:k + 1]: e_.scalar_tensor_tensor(
                    out=o, in0=a_, scalar=s_, in1=rstd3, op0=ALU.mult, op1=ALU.mult), ["p3xs", "p3rstd", "g1"], [("p3xb", k)], extra=extra)


        SBK = [[4, 5], [7, 6]]
        OBK = [0, 1, 2]

        NT_ATT = 0 if STOP < 4 else (1 if STOP < 5 else 4)
        allsteps = []
        pidx = 0
        for T in range(NT_ATT):
            steps = [(32, 16, 0, None)]
            for s_ in range(4 * T):
                steps.append((s_, 128, 0, None))
                steps.append((16 + s_, 128, 0, None))
            for i in range(4):
                steps.append((4 * T + i, 128, 128 * i, 0))
                steps.append((16 + 4 * T + i, 128, 128 * i, 1 + (i % 2)))
            for p in range(4):
                for si, st in enumerate(steps):
                    allsteps.append(dict(T=T, p=p, si=si, n=len(steps), st=st, pidx=pidx))
                pidx += 1

        def emit_scores(j):
            a = allsteps[j]
            T, p = a["T"], a["p"]
            kb, nk, q0, mi = a["st"]
            rp_, rk = Rp[T % 2], "Rp%d" % (T % 2)
            kc0 = META0 if kb == 32 else kb * 128
            kkey = ("KT", p, (kc0 // 512) * 512 if kb != 32 else META0)
            for e in range(2):
                sb = SBK[j % 2][e]
                mm(ps[sb][0:nk, q0:512], KT[e * 64:(e + 1) * 64, p, kc0:kc0 + nk], QT[e * 64:(e + 1) * 64, p, T * 512 + q0:(T + 1) * 512],
                   True, False, [kkey, ("QT", p, T)], ["ps%d" % sb])
            for e in range(2):
                sb = SBK[j % 2][e]
                mm(ps[sb][0:nk, q0:512], onesP[e * 64:(e + 1) * 64, 0:nk], rp_[e * 64:(e + 1) * 64, p, q0:512], False, mi is None,
                   [rk, "onesP"], ["ps%d" % sb])
            if mi is not None:
                for e in range(2):
                    sb = SBK[j % 2][e]
                    mm(ps[sb][0:nk, q0:q0 + 128], ident_bf, masks_bf[:, mi * 128:(mi + 1) * 128], False, True,
                       ["ident_bf", "masks"], ["ps%d" % sb])

        def obank(a, e):
            return OBK[(2 * a["pidx"] + e) % 3]

        def emit_exp_pv(j):
            a = allsteps[j]
            T, p = a["T"], a["p"]
            kb, nk, q0, mi = a["st"]
            for e in range(2):
                sb = SBK[j % 2][e]
                pt, ptk = PT[j % 2][e], "PT%d_%d" % (j % 2, e)
                act(pt[0:nk, q0:512], ps[sb][0:nk, q0:512], ACT.Exp, ["ps%d" % sb, "NC"], [ptk], bias=NC[0:nk, 2 * p + e, kb:kb + 1], scale=0.125)
            for e in range(2):
                pt, ptk = PT[j % 2][e], "PT%d_%d" % (j % 2, e)
                ob = obank(a, e)
                mm(ps[ob][:, q0:512], VP[0:nk, kb, p, e * 64:e * 64 + 128], pt[0:nk, q0:512], a["si"] == 0, a["si"] == a["n"] - 1,
                   [ptk, ("VP", kb, 0), ("VP", kb, 1), "VP"], ["ps%d" % ob])

        nrm = [0]

        def norm(a, final=False):
            T, p = a["T"], a["p"]
            for e in range(2):
                ob = obank(a, e)
                i = nrm[0] % 2
                nrm[0] += 1
                o0, d0 = (0, 64) if e == 0 else (64, 0)
                if final:
                    copy("vector", Dsb[i][o0:o0 + 64, :], ps[ob][d0:d0 + 64, :], ["ps%d" % ob], ["Dsb%d" % i])
                    copy("scalar", Osb[i][o0:o0 + 64, :], ps[ob][o0:o0 + 64, :], ["ps%d" % ob, "Dsb%d" % i], ["Osb%d" % i])
                    act(Dsb[i][o0:o0 + 64, :], Dsb[i][o0:o0 + 64, :], ACT.Ln, ["Dsb%d" % i], ["Dsb%d" % i])
                    act(Dsb[i][o0:o0 + 64, :], Dsb[i][o0:o0 + 64, :], ACT.Exp, ["Dsb%d" % i], ["Dsb%d" % i], scale=-1.0)
                    tt("gpsimd", OATT[o0:o0 + 64, p, T * 512:(T + 1) * 512], Osb[i][o0:o0 + 64, :], Dsb[i][o0:o0 + 64, :], ALU.mult,
                       ["Osb%d" % i, "Dsb%d" % i], [("OATT", p, T)])
                    continue
                copy("vector", Osb[i][o0:o0 + 64, :], ps[ob][o0:o0 + 64, :], ["ps%d" % ob], ["Osb%d" % i])
                copy("vector", Dsb[i][o0:o0 + 64, :], ps[ob][d0:d0 + 64, :], ["ps%d" % ob], ["Dsb%d" % i])
                P.op("vector", lambda e_, o=Dsb[i][o0:o0 + 64, :]: e_.reciprocal(out=o, in_=o), ["Dsb%d" % i], ["Dsb%d" % i])
                tt("gpsimd", OATT[o0:o0 + 64, p, T * 512:(T + 1) * 512], Osb[i][o0:o0 + 64, :], Dsb[i][o0:o0 + 64, :], ALU.mult,
                   ["Osb%d" % i, "Dsb%d" % i], [("OATT", p, T)])

        NS = len(allsteps)
        if NS:
            if not rgen0_done[0]:
                prep_attn_consts(ph1_fence)
                rgen(0)
            emit_scores(0)
        early = [None]
        fenceA = [None]
        for j in range(NS):
            a = allsteps[j]
            if NT_ATT == 4 and a["T"] == 3 and a["p"] == 3:
                if a["si"] == 0:
                    early[0] = P.fence_set()
                    p3_issue(0, extra=early[0])
                if a["si"] == 24:
                    p3_rstd(0, banks=(MB, OBK[2]), extra=early[0], on_dve=True)
            if j + 1 < NS:
                emit_scores(j + 1)
            emit_exp_pv(j)
            if a["p"] == 0 and a["si"] == 3 and a["T"] + 1 < NT_ATT and not (a["T"] == 0 and rgen0_done[0]):
                rgen(a["T"] + 1, part=1)
            offs = (0, 1, 2, 3) if a["n"] < 12 else (0, 2, 4, 6)
            if a["p"] == 2 and a["T"] + 1 < NT_ATT and (a["n"] - 1 - a["si"]) in offs:
                rgen(a["T"] + 1, part=2, pair=3 - offs.index(a["n"] - 1 - a["si"]))
            if a["si"] == a["n"] - 1:
                if j == NS - 1 and early[0] is not None and STOP >= 6:
                    fenceA[0] = P.fence_set()
                    p3_apply(0, extra=fenceA[0])
                    dma("gpsimd", PW, pool_w.rearrange("g c d -> c g d"), [], ["PW"], "w3_pw", extra=fenceA[0])
                    dma("gpsimd", WUA, w_up_attn.rearrange("(c p) n -> p c n", p=128), [], ["WUA"], "w3_ua", extra=fenceA[0])
                    dma("gpsimd", WUP, w_up_pool.rearrange("(c p) n -> p c n", p=128), [], ["WUP"], "w3_up", extra=fenceA[0])
                    dma("gpsimd", WO, w_out.rearrange("(k p) n -> p k n", p=128), [], ["WO"], "w3_wo", extra=fenceA[0])
                    norm(a, final=True)
                else:
                    norm(a)

        if DEBUG:
            dma("sync", dbg_kt, KT.rearrange("p c n -> p (c n)"), [("KT", p, c) for p in range(4) for c in [0, 512, 1024, 1536, 2048, 2560, 3072, 3584, META0]], [], "dbg")
            dma("sync", dbg_nc, NC.rearrange("p h b -> p (h b)"), ["NC"], [], "dbg")
            dma("sync", dbg_qt, QT.rearrange("p c n -> p (c n)"), [("QT", p, t) for p in range(4) for t in range(4)], [], "dbg")
            dma("sync", dbg_vp, VP.rearrange("p b c n -> p (b c n)"), ["VP"], [], "dbg")

        if STOP < 6:
            fin = P.fence_set()
            P.wait_only("sync", fin)
            P.emit(nc)
            return nc
        fence = P.fence_set()
        if fenceA[0] is not None:
            for e in ENGS:
                P.wait_only(e, fenceA[0])
        else:
            for e in ENGS:
                P.wait_only(e, fence)
        if DEBUG:
            dma("sync", dbg_oatt, OATT.rearrange("p c n -> p (c n)"), [], [], "dbg")
        if fenceA[0] is None:
            dma("gpsimd", PW, pool_w.rearrange("g c d -> c g d"), [], ["PW"], "w3_pw")
            dma("gpsimd", WUA, w_up_attn.rearrange("(c p) n -> p c n", p=128), [], ["WUA"], "w3_ua")
            dma("gpsimd", WUP, w_up_pool.rearrange("(c p) n -> p c n", p=128), [], ["WUP"], "w3_up")
            dma("gpsimd", WO, w_out.rearrange("(k p) n -> p k n", p=128), [], ["WO"], "w3_wo")

        def proj(col, n0, n):
            b = bank()
            for k in range(8):
                mm(ps[b][:, 0:n], W2[:, k, col:col + 128], xb3[:, k, n0:n0 + n], k == 0, k == 7, [("p3xb", k), w2key(col)], ["ps%d" % b])
            return b

        def gate2(b):
            i = thc[0] % 2
            thc[0] += 1
            act(TTt[i], ps[b][:, :], ACT.Silu, ["ps%d" % b], ["TT%d" % i])
            return TTt[i], "TT%d" % i

        if early[0] is None:
            p3_issue(0)
            p3_rstd(0)
        if fenceA[0] is None:
            p3_apply(0)
        bank_first.extend([4, 5, 6, 7, 2, 3])
        pb[0] = 0
        for T in range(4):
            if T + 1 < 4:
                p3_issue(T + 1)
            for blk in range(4):
                row0 = T * 512 + blk * 128
                dma("sync", RES[blk], x_own[row0:row0 + 128, :], [], [("RES", blk)], "res%d" % blk, extra=(fence if T == 0 else ()))
            def uproj(g):
                bm = proj(g * 128, 0, 512)
                bh = proj(g * 128, 512, 64)
                uk = "U%d" % g
                copy("scalar", U[g][:, :, 16:144], ps[bm][:, :].rearrange("p (b n) -> p b n", b=4), ["ps%d" % bm], [uk])
                copy("scalar", U[g][:, :, 0:16], ps[bh][:, 0:64].rearrange("p (b n) -> p b n", b=4), ["ps%d" % bh], [uk])

            def chain_pool(g):
                src, srck = U[g], "U%d" % g
                bufs = [(S1, "S1"), (S2, "S2")]
                w = 1
                lo = 0
                for lvl in range(g + 1):
                    dst, dstk = bufs[lvl % 2]
                    lo2 = lo + w
                    tt("gpsimd", dst[:, :, lo2:144], src[:, :, lo2:144], src[:, :, lo2 - w:144 - w], ALU.add, [srck], [dstk])
                    src, srck, lo, w = dst, dstk, lo2, 2 * w
                return src, srck, w

            def chain_fin(g, c3):
                src, srck, w = c3
                stt("vector", PL[g].rearrange("p (b n) -> p b n", b=4), U[g][:, :, 16:144], -float(w), src[:, :, 16:144], ALU.mult, ALU.add,
                    [srck, "U%d" % g], ["PL%d" % g])

            def zattn(c):
                bz = proj(1024 + c * 128, 0, 512)
                t2, t2k = gate2(bz)
                tt("vector", YA[:, c, :], OATT[:, c, T * 512:(T + 1) * 512], t2, ALU.mult, [t2k, ("OATT", c, T)], [("YA", c)])

            uproj(3)
            c3 = chain_pool(3)
            uproj(2)
            uproj(1)
            uproj(0)
            chain_fin(3, c3)
            c2 = chain_pool(2)
            zattn(0)
            zattn(1)
            chain_fin(2, c2)
            c1 = chain_pool(1)
            zattn(2)
            chain_fin(1, c1)
            c0 = chain_pool(0)
            zattn(3)
            chain_fin(0, c0)
            def pool_fin(g, t2, t2k):
                by = bank()
                mm(ps[by][:, :], PW[:, g, :], PL[g], True, True, ["PL%d" % g, "PW"], ["ps%d" % by])
                stt("vector", YP[:, g, :], ps[by][:, :], hps[:, g:g + 1], t2, ALU.mult, ALU.mult, ["ps%d" % by, t2k, "hps"], [("YP", g)])
            pend = []
            for g in (3, 2, 1, 0):
                bz = proj(512 + g * 128, 0, 512)
                t2, t2k = gate2(bz)
                if pend and g >= 1:
                    pool_fin(*pend.pop(0))
                pend.append((g, t2, t2k))
            for m in range(8):
                nxt = T + 1 < 4
                if nxt and m == 1:
                    rb = (bank(), bank())
                    reserved.update(rb)
                if nxt and 1 <= m <= 4:
                    p3_rstd(T + 1, banks=rb, ks=range(2 * (m - 1), 2 * m), do_mm=False, fin=False)
                bgp = proj(1536 + m * 128, 0, 512)
                bga = proj(2560 + m * 128, 0, 512)
                while pend:
                    pool_fin(*pend.pop(0))
                if nxt and 1 <= m <= 4:
                    p3_rstd(T + 1, banks=rb, ks=range(2 * (m - 1), 2 * m), do_sq=False, fin=False)
                bup = bank()
                for c in range(4):
                    mm(ps[bup][:, :], WUP[:, c, m * 128:(m + 1) * 128], YP[:, c, :], c == 0, c == 3, [("YP", c), "WUP"], ["ps%d" % bup])
                bua = bank()
                for c in range(4):
                    mm(ps[bua][:, :], WUA[:, c, m * 128:(m + 1) * 128], YA[:, c, :], c == 0, c == 3, [("YA", c), "WUA"], ["ps%d" % bua])
                i = thc[0] % 2
                thc[0] += 1
                act(TH[i], ps[bgp][:, :], ACT.Tanh, ["ps%d" % bgp], ["TH%d" % i], scale=0.5)
                stt("vector", M1, TH[i], 1.0, ps[bup][:, :], ALU.add, ALU.mult, ["TH%d" % i, "ps%d" % bup], ["M1"])
                i = thc[0] % 2
                thc[0] += 1
                act(TH[i], ps[bga][:, :], ACT.Tanh, ["ps%d" % bga], ["TH%d" % i], scale=0.5)
                stt("vector", M2, TH[i], 1.0, ps[bua][:, :], ALU.add, ALU.mult, ["TH%d" % i, "ps%d" % bua], ["M2"])
                tt("vector" if m == 7 else "gpsimd", MG[:, m, :], M1, M2, ALU.add, ["M1", "M2"], [("MG", m)])
                if m == 5 and nxt:
                    p3_rstd(T + 1, banks=rb, ks=(), fin=True)
                    reserved.clear()
            if T + 1 < 4:
                p3_apply(T + 1)
            memset("vector", ss4, 0.0, ["ss4"])
            for blk in range(4):
                row0 = T * 512 + blk * 128
                sk = ("ss4", blk)
                for hf in range(2):
                    bo = bank()
                    for m in range(8):
                        mm(ps[bo][:, :], MG[:, m, blk * 128:(blk + 1) * 128], WO[:, m, hf * 512:(hf + 1) * 512], m == 0, m == 7,
                           [("MG", m), "WO"], ["ps%d" % bo])
                    stt("vector", RES[blk][:, hf * 512:(hf + 1) * 512], ps[bo][:, :], 0.5, RES[blk][:, hf * 512:(hf + 1) * 512], ALU.mult, ALU.add,
                        ["ps%d" % bo, ("RES", blk)], [("RES", blk)])
                act(PLj[:, 0:1024], RES[blk], ACT.Square, [("RES", blk), "ss4"], ["PL0", "PL1", sk], accum=ss4[:, blk:blk + 1])
                act(ss4[:, blk:blk + 1], ss4[:, blk:blk + 1], ACT.Ln, [sk, "eps_c"], [sk], bias=eps_c, scale=1.0 / 1024)
                act(ss4[:, blk:blk + 1], ss4[:, blk:blk + 1], ACT.Exp, [sk], [sk], scale=-0.5)
                for pb_ in ([blk - 1] if blk > 0 else []) + ([blk] if blk == 3 else []):
                    r0_ = T * 512 + pb_ * 128
                    stt("vector", RES[pb_], RES[pb_], ss4[:, pb_:pb_ + 1], fng_b, ALU.mult, ALU.mult, [("RES", pb_), ("ss4", pb_), "fng"], [("RES", pb_)])
                    dma("sync", out[r0_:r0_ + 128, :], RES[pb_], [("RES", pb_)], [], "out%d" % pb_)

        fin = P.fence_set()
        P.wait_only("sync", fin)
        P.emit(nc)
    return nc


def _core_layout(j):
    a = [2 * s + ((s % 2) if j == 0 else 1 - (s % 2)) for s in range(16)]
    b = [2 * s + (1 - (s % 2) if j == 0 else (s % 2)) for s in range(16)]
    return a, b


def make_in_maps(x, meta_tokens, norm_g, w_in, b_forget, pool_w, pool_scale, w_up_pool, w_up_attn, w_out, final_norm_g):
    f = lambda v: np.ascontiguousarray(np.asarray(v, dtype=np.float32))
    x = f(x)
    meta = f(meta_tokens)
    shared = {
        "w_in": f(w_in)[0], "pool_w": f(pool_w)[0], "w_up_pool": f(w_up_pool)[0], "w_up_attn": f(w_up_attn)[0],
        "w_out": f(w_out)[0],
        "ng": np.ascontiguousarray(f(norm_g)[0].reshape(8, 128).T),
        "pscale": np.ascontiguousarray(f(pool_scale)[0].reshape(4, 128).T),
        "bfg": f(b_forget)[0], "fng": f(final_norm_g),
    }
    kk = np.arange(128)[:, None]
    qq = np.arange(128)[None, :]
    tri = np.where(kk <= qq, 0.0, NEG).astype(np.float32)
    in_maps = []
    for core in range(8):
        bidx, j = core // 2, core % 2
        a, b = _core_layout(j)
        seq = np.concatenate([meta, x[bidx]], 0)
        parts = [seq[16 + g * 128:16 + (g + 1) * 128] for g in a]
        parts += [seq[16 + g * 128:16 + (g + 1) * 128] for g in b]
        parts.append(seq[0:16])
        parts += [seq[g * 128:g * 128 + 16] for g in a]
        xloc = np.concatenate(parts, 0)
        gpos = np.array([1 + g for g in a] + [1 + g for g in b] + [0])
        pmat = (gpos[:, None] < gpos[None, :]).astype(np.float32)
        full = np.full((128, 128), NEG, np.float32)
        zero = np.zeros((128, 128), np.float32)
        mk = np.concatenate([tri, full if j == 0 else zero, zero if j == 0 else full], 1)
        m = dict(shared)
        m["xT"] = np.ascontiguousarray(xloc.T)
        m["x_own"] = np.ascontiguousarray(xloc[0:2048])
        m["pm"] = np.ascontiguousarray(pmat)
        m["masks"] = np.ascontiguousarray(mk)
        in_maps.append(m)
    return in_maps


def kernel(x, meta_tokens, norm_g, w_in, b_forget, pool_w, pool_scale, w_up_pool, w_up_attn, w_out, final_norm_g):
    in_maps = make_in_maps(x, meta_tokens, norm_g, w_in, b_forget, pool_w, pool_scale, w_up_pool, w_up_attn, w_out, final_norm_g)
    nc = build_nc()
    res = run_bass_kernel_spmd(nc, in_maps, core_ids=list(range(8)))
    outp = np.zeros((4, 4096, 1024), np.float32)
    for core in range(8):
        bidx, j = core // 2, core % 2
        a, _ = _core_layout(j)
        o = np.asarray(res.results[core]["out"], dtype=np.float32)
        for s, g in enumerate(a):
            outp[bidx, g * 128:(g + 1) * 128] = o[s * 128:(s + 1) * 128]
    return outp
```

```python
from contextlib import ExitStack
import numpy as np
import concourse.bass as bass
import concourse.mybir as mybir
from concourse.bass_utils import run_bass_kernel_spmd

F32 = mybir.dt.float32
BF16 = mybir.dt.bfloat16
ALU = mybir.AluOpType
ACT = mybir.ActivationFunctionType

import os
DEBUG = False
STOP = int(os.environ.get("KSTOP", "99"))
SUB = int(os.environ.get("KSUB", "99"))
NEG = -240000.0
EPS = 1e-6
NTOK = 4368
META0 = 4096
HALO0 = 4112
ENGS = ["tensor", "vector", "scalar", "gpsimd", "sync"]
EPOCH = 1000


class Prog:
    def __init__(self):
        self.ops = {e: [] for e in ENGS}
        self.last_writer = {}
        self.readers = {}
        self.all_ops = []
        self.dma_counts = {}

    def op(self, eng, fn, reads=(), writes=(), dma=None, extra=()):
        deps = list(extra)
        for b in reads:
            w = self.last_writer.get(b)
            if w is not None:
                deps.append(w)
        for b in writes:
            w = self.last_writer.get(b)
            if w is not None:
                deps.append(w)
            deps.extend(self.readers.get(b, ()))
        rec = dict(eng=eng, fn=fn, deps=[], dma=dma, sig=False, id=len(self.all_ops))
        seen = set()
        raw = set(id(self.last_writer.get(b)) for b in reads)
        for d in deps:
            if d["id"] in seen:
                continue
            seen.add(d["id"])
            if d["eng"] == eng and d["dma"] is None and dma is None and eng != "gpsimd":
                if id(d) not in raw:
                    continue
            rec["deps"].append(d)
            d["sig"] = True
        if dma is not None:
            self.dma_counts[dma] = self.dma_counts.get(dma, 0) + 1
            rec["dma_n"] = self.dma_counts[dma]
        for b in reads:
            self.readers.setdefault(b, []).append(rec)
        for b in writes:
            self.last_writer[b] = rec
            self.readers[b] = []
        self.ops[eng].append(rec)
        self.all_ops.append(rec)
        return rec

    def wait_only(self, eng, recs):
        rec = dict(eng=eng, fn=None, deps=list(recs), dma=None, sig=False, id=len(self.all_ops))
        for d in recs:
            d["sig"] = True
        self.ops[eng].append(rec)
        self.all_ops.append(rec)
        return rec

    def fence_set(self):
        recs = []
        for e in ENGS:
            for r in reversed(self.ops[e]):
                if r["fn"] is not None and r["dma"] is None:
                    recs.append(r)
                    break
        last = {}
        for r in self.all_ops:
            if r["dma"] is not None:
                last[r["dma"]] = r
        recs.extend(last.values())
        return recs

    def emit(self, nc):
        nsem = {}
        for e in ENGS:
            n = 0
            for r in self.ops[e]:
                if r["dma"] is None and r["sig"] and r["fn"] is not None:
                    n += 1
                    r["sig_n"] = n
            nsem[e] = (n + EPOCH - 1) // EPOCH
        with ExitStack() as es:
            esem = {e: [es.enter_context(nc.semaphore(f"s_{e}_{i}")) for i in range(nsem[e])] for e in ENGS}
            dsem = {k: es.enter_context(nc.semaphore(f"d_{k}")) for k in self.dma_counts}
            block = es.enter_context(nc.Block())

            def make(e):
                def body(engine):
                    waited = {}
                    for r in self.ops[e]:
                        for d in r["deps"]:
                            if d["dma"] is not None:
                                sem, val, key = dsem[d["dma"]], 16 * d["dma_n"], ("d", d["dma"])
                            else:
                                ep, v = divmod(d["sig_n"] - 1, EPOCH)
                                sem, val, key = esem[d["eng"]][ep], v + 1, (d["eng"], ep)
                            if waited.get(key, 0) >= val:
                                continue
                            waited[key] = val
                            engine.wait_ge(sem, val)
                        if r["fn"] is None:
                            continue
                        inst = r["fn"](engine)
                        if r["dma"] is not None:
                            inst.then_inc(dsem[r["dma"]], 16)
                        elif r["sig"]:
                            ep, v = divmod(r["sig_n"] - 1, EPOCH)
                            inst.then_inc(esem[e][ep], 1)
                return body

            for e in ENGS:
                if self.ops[e]:
                    getattr(block, e)(make(e))


def build_nc():
    nc = bass.Bass("TRN2", target_bir_lowering=False)
    D = lambda name, shape, kind="ExternalInput": nc.dram_tensor(name, shape, F32, kind=kind).ap()
    xT = D("xT", [1024, NTOK])
    x_own = D("x_own", [2048, 1024])
    w_in = D("w_in", [1024, 5128])
    pool_w = D("pool_w", [4, 128, 128])
    w_up_pool = D("w_up_pool", [512, 1024])
    w_up_attn = D("w_up_attn", [512, 1024])
    w_out = D("w_out", [1024, 1024])
    ng = D("ng", [128, 8])
    pscale = D("pscale", [128, 4])
    bfg = D("bfg", [8])
    fng = D("fng", [1024])
    pm = D("pm", [33, 33])
    masks = D("masks", [128, 384])
    out = D("out", [2048, 1024], kind="ExternalOutput")
    if DEBUG:
        dbg_kt = D("dbg_kt", [128, 4 * 4112], kind="ExternalOutput")
        dbg_nc = D("dbg_nc", [128, 264], kind="ExternalOutput")
        dbg_oatt = D("dbg_oatt", [128, 4 * 2048], kind="ExternalOutput")
        dbg_qt = D("dbg_qt", [128, 4 * 2048], kind="ExternalOutput")
        dbg_vp = D("dbg_vp", [128, 33 * 768], kind="ExternalOutput")

    xT_v = xT.rearrange("(k p) n -> p k n", p=128)
    w_in_v = w_in.rearrange("(k p) n -> p k n", p=128)

    P = Prog()
    with ExitStack() as es:
        ARENA = 106000
        arena = es.enter_context(nc.sbuf_tensor("arena", [128, ARENA], BF16))
        ps = [es.enter_context(nc.psum_tensor(f"ps{i}", [128, 512], F32)) for i in range(8)]

        def view(off, nfree, dt, parts=128):
            assert off % 4 == 0
            nb = nfree * (4 if dt == F32 else 2)
            a = arena[0:parts, off // 2:(off + nb) // 2]
            if dt == F32:
                a = a.bitcast(F32)
            return a

        class Bump:
            def __init__(self, start, end):
                self.o, self.end = start, end

            def __call__(self, nfree, dt, parts=128):
                nb = nfree * (4 if dt == F32 else 2)
                nb = (nb + 63) // 64 * 64
                v = view(self.o, nfree, dt, parts)
                self.o += nb
                assert self.o <= self.end, (self.o, self.end)
                return v

        C0, C1 = 0, 8192
        O0, O1 = C1, C1 + 16384
        T0, T1 = O1, O1 + 57344
        A0, A1 = T1, T1 + 101056
        D0, D1 = A1, ARENA * 2
        cb = Bump(C0, C1)
        ident_bf = cb(128, BF16)
        ones_bf = cb(128, BF16)
        ones_f = cb(128, F32)
        tri_f = cb(128, F32)
        masks_bf = cb(384, BF16)
        g1 = cb(8, F32)
        hps = cb(4, F32)
        bfb = cb(8, F32)
        pm_sb = cb(33, F32, parts=33)
        fng_b = cb(1024, F32)
        eps_c = cb(1, F32)
        act_warm = cb(1, F32)
        ident_f = cb(128, F32)
        OATT = view(O0, 4 * 2048, BF16).rearrange("p (c n) -> p c n", c=4)

        ab = Bump(A0, A1)
        KT = ab(4 * 4112, BF16).rearrange("p (c n) -> p c n", c=4)
        QT = ab(4 * 2048, BF16).rearrange("p (c n) -> p c n", c=4)
        VP = ab(33 * 768, BF16).rearrange("p (b c n) -> p b c n", b=33, c=4)
        NC = ab(264, F32).rearrange("p (h b) -> p h b", h=8)

        tb = Bump(T0, T1)
        W1q = tb(8 * 512, BF16).rearrange("p (k n) -> p k n", k=8)
        W1k = tb(8 * 512, BF16).rearrange("p (k n) -> p k n", k=8)
        W1v = tb(8 * 512, BF16).rearrange("p (k n) -> p k n", k=8)
        W1f = tb(8 * 8, BF16).rearrange("p (k n) -> p k n", k=8)
        xs = tb(8 * 512, F32).rearrange("p (k n) -> p k n", k=8)
        xb = tb(8 * 512, BF16).rearrange("p (k n) -> p k n", k=8)
        rstd_b = tb(528, F32)
        NLF = tb(264, F32).rearrange("p (h b) -> p h b", h=8)
        ft0 = tb(32, F32).rearrange("p (b h) -> p b h", b=4)
        ft1 = tb(32, F32).rearrange("p (b h) -> p b h", b=4)
        ST = tb(8, F32, parts=33)
        STrep = tb(8 * 128, F32, parts=33).rearrange("p (h n) -> p h n", h=8)

        db = Bump(D0, D1)
        PT = [[db(512, BF16) for _ in range(2)] for _ in range(2)]
        Osb = [db(512, F32) for _ in range(2)]
        Dsb = [db(512, F32) for _ in range(2)]
        negones = db(512, F32)
        Rp = [db(4 * 512, BF16).rearrange("p (c n) -> p c n", c=4) for _ in range(2)]
        ZR = [db(16 * 65, BF16).rearrange("p (s c n) -> p s c n", s=4, c=4) for _ in range(2)]
        onesP = db(128, BF16)

        def mm(o, lhsT, rhs, start, stop, reads, writes, extra=()):
            return P.op("tensor", lambda e: e.matmul(o, lhsT=lhsT, rhs=rhs, start=start, stop=stop), reads, writes, extra=extra)

        def act(o, i, func, reads, writes, bias=None, scale=None, accum=None):
            kw = {}
            if bias is not None:
                kw["bias"] = bias
            if scale is not None:
                kw["scale"] = scale
            if accum is not None:
                kw["accum_out"] = accum
            return P.op("scalar", lambda e: e.activation(out=o, in_=i, func=func, **kw), reads, writes)

        def copy(eng, o, i, reads, writes):
            if eng == "scalar":
                return P.op("scalar", lambda e: e.activation(out=o, in_=i, func=ACT.Copy), reads, writes)
            return P.op(eng, lambda e: e.tensor_copy(out=o, in_=i), reads, writes)

        def tt(eng, o, a, b, op, reads, writes):
            return P.op(eng, lambda e: e.tensor_tensor(out=o, in0=a, in1=b, op=op), reads, writes)

        def stt(eng, o, a, s, b, op0, op1, reads, writes):
            return P.op(eng, lambda e: e.scalar_tensor_tensor(out=o, in0=a, scalar=s, in1=b, op0=op0, op1=op1), reads, writes)

        def ts(eng, o, a, s1, s2, op0, op1, reads, writes):
            if s2 is None:
                return P.op(eng, lambda e: e.tensor_scalar(out=o, in0=a, scalar1=s1, scalar2=None, op0=op0), reads, writes)
            return P.op(eng, lambda e: e.tensor_scalar(out=o, in0=a, scalar1=s1, scalar2=s2, op0=op0, op1=op1), reads, writes)

        def memset(eng, o, val, writes):
            return P.op(eng, lambda e: e.memset(o, val), (), writes)

        def dma(q, o, i, reads, writes, key, extra=()):
            return P.op(q, lambda e: e.dma_start(out=o, in_=i), reads, writes, dma=key, extra=extra)

        dma("sync", g1, ng, [], ["g1"], "c_g1")
        dma("sync", bfb, bfg.partition_broadcast(128), [], ["bfb"], "c_bfb")
        memset("vector", ones_bf, 1.0, ["ones_bf"])
        memset("vector", ones_f, 1.0, ["ones_f"])
        memset("vector", eps_c, EPS, ["eps_c"])
        act(act_warm, eps_c, ACT.Ln, ["eps_c"], ["act_warm"])
        memset("vector", NLF, 0.0, ["NLF"])
        memset("gpsimd", tri_f, 1.0, ["tri_f"])
        P.op("gpsimd", lambda e: e.affine_select(out=tri_f, in_=tri_f, pattern=[[1, 128]], compare_op=ALU.is_ge,
                                                 fill=0.0, base=0, channel_multiplier=-1), ["tri_f"], ["tri_f"])
        memset("gpsimd", ident_f, 1.0, ["ident_f"])
        P.op("gpsimd", lambda e: e.affine_select(out=ident_f, in_=ident_f, pattern=[[1, 128]], compare_op=ALU.is_ge,
                                                 fill=0.0, base=0, channel_multiplier=-1), ["ident_f"], ["ident_f"])
        P.op("gpsimd", lambda e: e.affine_select(out=ident_f, in_=ident_f, pattern=[[-1, 128]], compare_op=ALU.is_ge,
                                                 fill=0.0, base=0, channel_multiplier=1), ["ident_f"], ["ident_f"])

        pb = [0]

        bank_first = []

        reserved = set()

        def bank():
            if bank_first:
                return bank_first.pop(0)
            while True:
                i = pb[0]
                pb[0] = (pb[0] + 1) % 8
                if i not in reserved:
                    return i

        def issue_x(xs_t, col0, n, xsk, dkey, sub=(), extra=()):
            r = dma("sync", xs_t[:, :, 0:n], xT_v[:, :, col0:col0 + n], [], [xsk], dkey, extra=extra)
            for (c0, cn, off) in sub:
                r = dma("sync", xs_t[:, :, off:off + cn], xT_v[:, :, c0:c0 + cn], [], [xsk], dkey, extra=extra)
            return r

        def norm_sq(xs_t, xb_t, ntot, xsk, xbk):
            for h_ in range(2):
                act(xb_t[:, 4 * h_:4 * h_ + 4, 0:ntot], xs_t[:, 4 * h_:4 * h_ + 4, 0:ntot], ACT.Square, [xsk],
                    [(xbk, k_) for k_ in range(4 * h_, 4 * h_ + 4)])

        def norm_x(xs_t, xb_t, rstd_t, ntot, xsk, xbk, rk, sq=True):
            if sq:
                norm_sq(xs_t, xb_t, ntot, xsk, xbk)
            done = 0
            while done < ntot:
                cn = min(512, ntot - done)
                b = bank()
                for k in range(8):
                    mm(ps[b][:, 0:cn], ones_bf, xb_t[:, k, done:done + cn], k == 0, k == 7, [(xbk, k), "ones_bf"], ["ps%d" % b])
                act(rstd_t[:, done:done + cn], ps[b][:, 0:cn], ACT.Ln, ["ps%d" % b, "eps_c"], [rk], bias=eps_c, scale=1.0 / 1024)
                done += cn
            act(rstd_t[:, 0:ntot], rstd_t[:, 0:ntot], ACT.Exp, [rk], [rk], scale=-0.5)
            for k in range(8):
                stt("vector", xb_t[:, k, 0:ntot], xs_t[:, k, 0:ntot], g1[:, k:k + 1], rstd_t[:, 0:ntot],
                    ALU.mult, ALU.mult, [xsk, rk, "g1"], [(xbk, k)])

        def load_norm(xs_t, xb_t, rstd_t, col0, n, tag, sub=()):
            issue_x(xs_t, col0, n, tag + "xs", tag + "x", sub)
            norm_x(xs_t, xb_t, rstd_t, n + sum(s_[1] for s_ in sub), tag + "xs", tag + "xb", tag + "rstd")

        tiles = []
        for t in range(4):
            tiles.append((2048 + t * 512, 512, 16 + 4 * t, False, -1))
        for t in range(4):
            tiles.append((t * 512, 512, 4 * t, True, t))
        ev = [0]

        def evac_eng():
            ev[0] += 1
            return "scalar" if ev[0] % 2 == 0 else "vector"

        if STOP < 1:
            tiles = []
        elif STOP < 2:
            tiles = tiles[0:1]
        xs2 = view(D0, 8 * 528, F32).rearrange("p (k n) -> p k n", k=8)
        xb2 = view(D0 + 16896, 8 * 528, BF16).rearrange("p (k n) -> p k n", k=8)
        XS = [xs2, xs]
        XB = [xb2, xb]
        NTL = len(tiles)

        def p1_n(ti):
            return tiles[ti][1] + (16 if ti == 0 else 0)

        def p1_issue(ti, extra=()):
            return issue_x(XS[ti % 2], tiles[ti][0], tiles[ti][1], "p1xs%d" % (ti % 2), "p1x%d" % (ti % 2),
                           sub=([(META0, 16, 512)] if ti == 0 else ()), extra=extra)

        def p1_sq(ti):
            norm_sq(XS[ti % 2], XB[ti % 2], p1_n(ti), "p1xs%d" % (ti % 2), "p1xb%d" % (ti % 2))

        def p1_norm(ti, sq=True):
            norm_x(XS[ti % 2], XB[ti % 2], rstd_b, p1_n(ti), "p1xs%d" % (ti % 2), "p1xb%d" % (ti % 2), "p1rstd", sq=sq)

        cs_bw = [None]

        def emit_cumsum(part):
            if part == 1:
                bw = cs_bw[0]
                for h in range(8):
                    mm(ps[bw][:, h * 33:(h + 1) * 33], STrep[:, h, :], pm_sb, False, h == 7, ["STrep", "pm"], ["ps%d" % bw])
                copy("vector", NC.rearrange("p h b -> p (h b)"), ps[bw][:, 0:264], ["ps%d" % bw], ["NC"])
                return
            bw = bank()
            cs_bw[0] = bw
            NLF2 = NLF.rearrange("p h b -> p (h b)")
            mm(ps[bw][:, 0:264], tri_f, NLF2, True, False, ["NLF", "tri_f"], ["ps%d" % bw])
            bs = bank()
            for h in range(8):
                mm(ps[bs][0:33, h:h + 1], NLF[:, h, :], ones_f[:, 0:1], True, True, ["NLF", "ones_f"], ["ps%d" % bs])
            copy("vector", ST, ps[bs][0:33, 0:8], ["ps%d" % bs], ["ST"])
            copy("vector", STrep, ST.unsqueeze(2).to_broadcast([33, 8, 128]), ["ST"], ["STrep"])

        MB = 3

        def rgen(T, part=0, pair=None):
            zr = ZR[T % 2]
            rp_ = Rp[T % 2]
            zk = "ZR%d" % (T % 2)
            rk = "Rp%d" % (T % 2)
            if part in (0, 1):
                ts("vector", zr[:, :, :, 0:65:64], NC[:, :, 4 * T:4 * T + 4].rearrange("p (c e) s -> p s c e", c=4), -8.0, None,
                   ALU.mult, ALU.bypass, ["NC"], [zk])
            if part == 1:
                return
            for p in (range(4) if pair is None else [pair]):
                for s_ in range(4):
                    mm(ps[MB][0:65, s_ * 128:(s_ + 1) * 128], zr[:, s_, p, :], ident_bf, True, True, [zk, "ident_bf"], ["ps%d" % MB])
                copy("vector", rp_[0:65, p, :], ps[MB][0:65, :], ["ps%d" % MB], [rk])

        def prep_attn_consts(extra):
            for i_, z in enumerate(ZR):
                P.op("gpsimd", lambda e_, z=z: e_.memset(z, 0.0), (), ["ZR%d" % i_], extra=extra)
            for i_, z in enumerate(Rp):
                P.op("gpsimd", lambda e_, z=z: e_.memset(z, 0.0), (), ["Rp%d" % i_], extra=extra)
            P.op("gpsimd", lambda e_: e_.memset(onesP, 0.0), (), ["onesP"], extra=extra)
            memset("gpsimd", negones, -1.0, ["negones"])
            memset("gpsimd", onesP[0:1, :], 1.0, ["onesP"])
            memset("gpsimd", onesP[64:65, :], 1.0, ["onesP"])

        r_x0 = p1_issue(0)
        r_wf = dma("gpsimd", W1f, w_in_v[:, :, 3072:3080], [], ["W1f"], "w1f", extra=[r_x0])
        r_wk = dma("gpsimd", W1k, w_in_v[:, :, 1536:2048], [], ["W1k"], "w1k", extra=[r_x0])
        memset("gpsimd", VP[:, :, :, 64:128], 1.0, ["VP"])
        dma("gpsimd", W1v, w_in_v[:, :, 2048:2560], [], ["W1v"], "w1v", extra=[r_wk])
        dma("sync", hps, pscale, [], ["hps"], "c_hps")
        dma("sync", pm_sb, pm, [], ["pm"], "c_pm")
        dma("sync", fng_b, fng.partition_broadcast(128), [], ["fng"], "c_fng")
        if NTL > 1:
            p1_issue(1, extra=[r_wk])
        dma("gpsimd", W1q, w_in_v[:, :, 1024:1536], [], ["W1q"], "w1q", extra=[r_wk])
        dma("gpsimd", masks_bf, masks, [], ["masks"], "c_masks")
        if NTL:
            p1_norm(0)
        fence7 = [None]
        rgen0_done = [False]

        def tile_parts(xbt, xbk, coff, n, lb0, col0, ot):
            nblk = (n + 127) // 128

            def f_path():
                bf_ = bank()
                for i in range(nblk):
                    m = min(128, n - i * 128)
                    for k in range(8):
                        mm(ps[bf_][0:m, i * 8:(i + 1) * 8], xbt[:, k, coff + i * 128:coff + i * 128 + m], W1f[:, k, :], k == 0, k == 7,
                           [(xbk, k), "W1f"], ["ps%d" % bf_])
                m = min(128, n)
                psf = ps[bf_][0:m, 0:nblk * 8].rearrange("p (b h) -> p b h", h=8)
                tt("vector", ft0[0:m, 0:nblk, :], psf, bfb[0:m, :].unsqueeze(1).to_broadcast([m, nblk, 8]), ALU.add,
                   ["ps%d" % bf_, "bfb"], ["ft0"])
                act(ft1[0:m, 0:nblk, :], ft0[0:m, 0:nblk, :], ACT.Exp, ["ft0"], ["ft1"], scale=-1.0)
                act(NLF[0:m, :, lb0:lb0 + nblk].rearrange("p h b -> p b h"), ft1[0:m, 0:nblk, :], ACT.Ln, ["ft1"], ["NLF"], bias=1.0)

            def k_proj():
                for p in range(4):
                    b = bank()
                    for k in range(8):
                        mm(ps[b][:, 0:n], W1k[:, k, p * 128:(p + 1) * 128], xbt[:, k, coff:coff + n], k == 0, k == 7, [(xbk, k), "W1k"], ["ps%d" % b])
                    copy(evac_eng(), KT[:, p, col0:col0 + n], ps[b][:, 0:n], ["ps%d" % b], [("KT", p, col0)])

            def v_proj():
                for i in range(nblk):
                    m = min(128, n - i * 128)
                    b = bank()
                    for k in range(8):
                        mm(ps[b][0:m, :], xbt[:, k, coff + i * 128:coff + i * 128 + m], W1v[:, k, :], k == 0, k == 7, [(xbk, k), "W1v"], ["ps%d" % b])
                    pv = ps[b][0:m, :].rearrange("p (c n) -> p c n", c=4)
                    ve = evac_eng()
                    copy(ve, VP[0:m, lb0 + i, :, 0:64], pv[:, :, 0:64], ["ps%d" % b, "VP"], [("VP", lb0 + i, 0)])
                    copy(ve, VP[0:m, lb0 + i, :, 128:192], pv[:, :, 64:128], ["ps%d" % b, "VP"], [("VP", lb0 + i, 1)])

            def q_proj():
                for p in range(4):
                    b = bank()
                    for k in range(8):
                        mm(ps[b][:, 0:n], W1q[:, k, p * 128:(p + 1) * 128], xbt[:, k, coff:coff + n], k == 0, k == 7, [(xbk, k), "W1q"], ["ps%d" % b])
                    copy(evac_eng(), QT[:, p, col0:col0 + n], ps[b][:, 0:n], ["ps%d" % b], [("QT", p, ot)])

            return f_path, k_proj, v_proj, q_proj

        for ti, (col0, n, lb0, own, ot) in enumerate(tiles):
            if ti == NTL - 1 and NTL == 8 and STOP >= 4:
                fence7[0] = P.fence_set()
                prep_attn_consts(fence7[0])
            xbt = XB[ti % 2]
            xbk = "p1xb%d" % (ti % 2)
            if ti + 1 < NTL:
                p1_sq(ti + 1)
            f_path, k_proj, v_proj, q_proj = tile_parts(xbt, xbk, 0, n, lb0, col0, ot)
            f_path()
            k_proj()
            if ti + 1 < NTL:
                p1_norm(ti + 1, sq=False)
            if ti + 2 < NTL:
                p1_issue(ti + 2)
            if ti == 0:
                mf, mk, mv, _ = tile_parts(xbt, xbk, 512, 16, 32, META0, -1)
                mf()
                mk()
            if ti == NTL - 1 and STOP >= 3:
                emit_cumsum(0)
            v_proj()
            if ti == 0:
                mv()
            if ti == NTL - 1 and STOP >= 3:
                emit_cumsum(1)
                if STOP >= 4 and fence7[0] is not None:
                    rgen(0)
                    rgen(1, part=1)
                    rgen0_done[0] = True
            if own:
                q_proj()
            if ti == 0:
                copy("vector", ident_bf, ident_f, ["ident_f"], ["ident_bf"])
                for g_ in range(4):
                    ts("vector", hps[:, g_:g_ + 1], hps[:, g_:g_ + 1], 1.0 / (2 ** (g_ + 1)), None, ALU.mult, ALU.bypass, ["hps"], ["hps"])

        if STOP < 3:
            fin = P.fence_set()
            P.wait_only("sync", fin)
            P.emit(nc)
            return nc
        ph1_fence = P.fence_set()

        W2 = view(T0, 8 * 3584, BF16).rearrange("p (k n) -> p k n", k=8)
        w2_srcs = [(0, 1024, 0), (2560, 512, 1024), (3080, 1024, 1536), (4104, 1024, 2560)]
        for wi, (c0, cn, d0) in enumerate(w2_srcs):
            dma("gpsimd", W2[:, :, d0:d0 + cn], w_in_v[:, :, c0:c0 + cn], [], [("W2", wi)], "w2_%d" % wi, extra=ph1_fence)

        def w2key(col):
            for wi, (c0, cn, d0) in enumerate(w2_srcs):
                if d0 <= col < d0 + cn:
                    return ("W2", wi)

        p3 = Bump(A0, D1)
        xs3 = p3(8 * 576, F32).rearrange("p (k n) -> p k n", k=8)
        sqk = [p3(576, BF16) for _ in range(2)]
        rstd3 = p3(576, F32)
        assert p3.o <= A0 + 3 * 8224
        WUP = p3(4 * 1024, BF16).rearrange("p (c n) -> p c n", c=4)
        WUA = p3(4 * 1024, BF16).rearrange("p (c n) -> p c n", c=4)
        WO = p3(8 * 1024, BF16).rearrange("p (k n) -> p k n", k=8)
        PW = p3(4 * 128, BF16).rearrange("p (g n) -> p g n", g=4)
        xb3 = p3(8 * 576, BF16).rearrange("p (k n) -> p k n", k=8)
        U = [p3(4 * 144, F32).rearrange("p (b n) -> p b n", b=4) for _ in range(4)]
        S1 = p3(4 * 144, F32).rearrange("p (b n) -> p b n", b=4)
        S2 = p3(4 * 144, F32).rearrange("p (b n) -> p b n", b=4)
        PLj = p3(2048, BF16)
        PL = [PLj[:, i_ * 512:(i_ + 1) * 512] for i_ in range(4)]
        YP = p3(4 * 512, BF16).rearrange("p (c n) -> p c n", c=4)
        YA = p3(4 * 512, BF16).rearrange("p (c n) -> p c n", c=4)
        TH = [p3(512, F32) for _ in range(2)]
        TTt = [p3(512, F32) for _ in range(2)]
        M1 = p3(512, F32)
        M2 = p3(512, F32)
        MG = p3(8 * 512, BF16).rearrange("p (k n) -> p k n", k=8)
        RES = [p3(1024, F32) for _ in range(4)]
        ss4 = p3(4, F32)

        thc = [0]

        def p3_issue(T, extra=()):
            issue_x(xs3, T * 512, 512, "p3xs", "p3x", sub=[(HALO0 + 64 * T, 64, 512)], extra=extra)

        def p3_rstd(T, banks=None, extra=(), on_dve=False, ks=range(8), do_sq=True, do_mm=True, fin=True):
            bA, bB = banks if banks is not None else (bank(), bank())
            for k in ks:
                sq = sqk[k % 2]
                if not do_sq:
                    pass
                elif on_dve:
                    P.op("vector", lambda e_, o=sq, i=xs3[:, k, :]: e_.tensor_tensor(out=o, in0=i, in1=i, op=ALU.mult), ["p3xs"], ["sqk%d" % (k % 2)], extra=extra)
                else:
                    P.op("scalar", lambda e_, o=sq, i=xs3[:, k, :]: e_.activation(out=o, in_=i, func=ACT.Square), ["p3xs"], ["sqk%d" % (k % 2)], extra=extra)
                if do_mm:
                    mm(ps[bA][:, 0:512], ones_bf, sq[:, 0:512], k == 0, k == 7, ["sqk%d" % (k % 2), "ones_bf"], ["ps%d" % bA])
                    mm(ps[bB][:, 0:64], ones_bf, sq[:, 512:576], k == 0, k == 7, ["sqk%d" % (k % 2), "ones_bf"], ["ps%d" % bB])
            if not fin:
                return
            act(rstd3[:, 0:512], ps[bA][:, 0:512], ACT.Ln, ["ps%d" % bA, "eps_c"], ["p3rstd"], bias=eps_c, scale=1.0 / 1024)
            act(rstd3[:, 512:576], ps[bB][:, 0:64], ACT.Ln, ["ps%d" % bB, "eps_c"], ["p3rstd"], bias=eps_c, scale=1.0 / 1024)
            act(rstd3, rstd3, ACT.Exp, ["p3rstd"], ["p3rstd"], scale=-0.5)

        def p3_apply(T, extra=()):
            for k in range(8):
                P.op("vector", lambda e_, o=xb3[:, k, :], a_=xs3[:, k, :], s_=g1[:, k:k + 1]: e_.scalar_tensor_tensor(
                    out=o, in0=a_, scalar=s_, in1=rstd3, op0=ALU.mult, op1=ALU.mult), ["p3xs", "p3rstd", "g1"], [("p3xb", k)], extra=extra)


        SBK = [[4, 5], [7, 6]]
        OBK = [0, 1, 2]

        NT_ATT = 0 if STOP < 4 else (1 if STOP < 5 else 4)
        allsteps = []
        pidx = 0
        for T in range(NT_ATT):
            steps = [(32, 16, 0, None)]
            for s_ in range(4 * T):
                steps.append((s_, 128, 0, None))
                steps.append((16 + s_, 128, 0, None))
            for i in range(4):
                steps.append((4 * T + i, 128, 128 * i, 0))
                steps.append((16 + 4 * T + i, 128, 128 * i, 1 + (i % 2)))
            for p in range(4):
                for si, st in enumerate(steps):
                    allsteps.append(dict(T=T, p=p, si=si, n=len(steps), st=st, pidx=pidx))
                pidx += 1

        def emit_scores(j):
            a = allsteps[j]
            T, p = a["T"], a["p"]
            kb, nk, q0, mi = a["st"]
            rp_, rk = Rp[T % 2], "Rp%d" % (T % 2)
            kc0 = META0 if kb == 32 else kb * 128
            kkey = ("KT", p, (kc0 // 512) * 512 if kb != 32 else META0)
            for e in range(2):
                sb = SBK[j % 2][e]
                mm(ps[sb][0:nk, q0:512], KT[e * 64:(e + 1) * 64, p, kc0:kc0 + nk], QT[e * 64:(e + 1) * 64, p, T * 512 + q0:(T + 1) * 512],
                   True, False, [kkey, ("QT", p, T)], ["ps%d" % sb])
            for e in range(2):
                sb = SBK[j % 2][e]
                mm(ps[sb][0:nk, q0:512], onesP[e * 64:(e + 1) * 64, 0:nk], rp_[e * 64:(e + 1) * 64, p, q0:512], False, mi is None,
                   [rk, "onesP"], ["ps%d" % sb])
            if mi is not None:
                for e in range(2):
                    sb = SBK[j % 2][e]
                    mm(ps[sb][0:nk, q0:q0 + 128], ident_bf, masks_bf[:, mi * 128:(mi + 1) * 128], False, True,
                       ["ident_bf", "masks"], ["ps%d" % sb])

        def obank(a, e):
            return OBK[(2 * a["pidx"] + e) % 3]

        def emit_exp_pv(j):
            a = allsteps[j]
            T, p = a["T"], a["p"]
            kb, nk, q0, mi = a["st"]
            for e in range(2):
                sb = SBK[j % 2][e]
                pt, ptk = PT[j % 2][e], "PT%d_%d" % (j % 2, e)
                act(pt[0:nk, q0:512], ps[sb][0:nk, q0:512], ACT.Exp, ["ps%d" % sb, "NC"], [ptk], bias=NC[0:nk, 2 * p + e, kb:kb + 1], scale=0.125)
            for e in range(2):
                pt, ptk = PT[j % 2][e], "PT%d_%d" % (j % 2, e)
                ob = obank(a, e)
                mm(ps[ob][:, q0:512], VP[0:nk, kb, p, e * 64:e * 64 + 128], pt[0:nk, q0:512], a["si"] == 0, a["si"] == a["n"] - 1,
                   [ptk, ("VP", kb, 0), ("VP", kb, 1), "VP"], ["ps%d" % ob])

        nrm = [0]

        def norm(a, final=False):
            T, p = a["T"], a["p"]
            for e in range(2):
                ob = obank(a, e)
                i = nrm[0] % 2
                nrm[0] += 1
                o0, d0 = (0, 64) if e == 0 else (64, 0)
                if final:
                    copy("vector", Dsb[i][o0:o0 + 64, :], ps[ob][d0:d0 + 64, :], ["ps%d" % ob], ["Dsb%d" % i])
                    copy("scalar", Osb[i][o0:o0 + 64, :], ps[ob][o0:o0 + 64, :], ["ps%d" % ob, "Dsb%d" % i], ["Osb%d" % i])
                    act(Dsb[i][o0:o0 + 64, :], Dsb[i][o0:o0 + 64, :], ACT.Ln, ["Dsb%d" % i], ["Dsb%d" % i])
                    act(Dsb[i][o0:o0 + 64, :], Dsb[i][o0:o0 + 64, :], ACT.Exp, ["Dsb%d" % i], ["Dsb%d" % i], scale=-1.0)
                    tt("gpsimd", OATT[o0:o0 + 64, p, T * 512:(T + 1) * 512], Osb[i][o0:o0 + 64, :], Dsb[i][o0:o0 + 64, :], ALU.mult,
                       ["Osb%d" % i, "Dsb%d" % i], [("OATT", p, T)])
                    continue
                copy("vector", Osb[i][o0:o0 + 64, :], ps[ob][o0:o0 + 64, :], ["ps%d" % ob], ["Osb%d" % i])
                copy("vector", Dsb[i][o0:o0 + 64, :], ps[ob][d0:d0 + 64, :], ["ps%d" % ob], ["Dsb%d" % i])
                P.op("vector", lambda e_, o=Dsb[i][o0:o0 + 64, :]: e_.reciprocal(out=o, in_=o), ["Dsb%d" % i], ["Dsb%d" % i])
                tt("gpsimd", OATT[o0:o0 + 64, p, T * 512:(T + 1) * 512], Osb[i][o0:o0 + 64, :], Dsb[i][o0:o0 + 64, :], ALU.mult,
                   ["Osb%d" % i, "Dsb%d" % i], [("OATT", p, T)])

        NS = len(allsteps)
        if NS:
            if not rgen0_done[0]:
                prep_attn_consts(ph1_fence)
                rgen(0)
            emit_scores(0)
        early = [None]
        fenceA = [None]
        for j in range(NS):
            a = allsteps[j]
            if NT_ATT == 4 and a["T"] == 3 and a["p"] == 3:
                if a["si"] == 0:
                    early[0] = P.fence_set()
                    p3_issue(0, extra=early[0])
                if a["si"] == 24:
                    p3_rstd(0, banks=(MB, OBK[2]), extra=early[0], on_dve=True, fin=False)
                if a["si"] == 30:
                    p3_rstd(0, banks=(MB, OBK[2]), extra=early[0], ks=(), fin=True)
            if j + 1 < NS:
                emit_scores(j + 1)
            emit_exp_pv(j)
            if a["p"] == 0 and a["si"] == 3 and a["T"] + 1 < NT_ATT and not (a["T"] == 0 and rgen0_done[0]):
                rgen(a["T"] + 1, part=1)
            offs = (0, 1, 2, 3) if a["n"] < 12 else (0, 2, 4, 6)
            if a["p"] == 2 and a["T"] + 1 < NT_ATT and (a["n"] - 1 - a["si"]) in offs:
                rgen(a["T"] + 1, part=2, pair=3 - offs.index(a["n"] - 1 - a["si"]))
            if a["si"] == a["n"] - 1:
                if j == NS - 1 and early[0] is not None and STOP >= 6:
                    fenceA[0] = P.fence_set()
                    p3_apply(0, extra=fenceA[0])
                    dma("gpsimd", PW, pool_w.rearrange("g c d -> c g d"), [], ["PW"], "w3_pw", extra=fenceA[0])
                    dma("gpsimd", WUA, w_up_attn.rearrange("(c p) n -> p c n", p=128), [], ["WUA"], "w3_ua", extra=fenceA[0])
                    dma("gpsimd", WUP, w_up_pool.rearrange("(c p) n -> p c n", p=128), [], ["WUP"], "w3_up", extra=fenceA[0])
                    dma("gpsimd", WO, w_out.rearrange("(k p) n -> p k n", p=128), [], ["WO"], "w3_wo", extra=fenceA[0])
                    norm(a, final=True)
                else:
                    norm(a)

        if DEBUG:
            dma("sync", dbg_kt, KT.rearrange("p c n -> p (c n)"), [("KT", p, c) for p in range(4) for c in [0, 512, 1024, 1536, 2048, 2560, 3072, 3584, META0]], [], "dbg")
            dma("sync", dbg_nc, NC.rearrange("p h b -> p (h b)"), ["NC"], [], "dbg")
            dma("sync", dbg_qt, QT.rearrange("p c n -> p (c n)"), [("QT", p, t) for p in range(4) for t in range(4)], [], "dbg")
            dma("sync", dbg_vp, VP.rearrange("p b c n -> p (b c n)"), ["VP"], [], "dbg")

        if STOP < 6:
            fin = P.fence_set()
            P.wait_only("sync", fin)
            P.emit(nc)
            return nc
        fence = P.fence_set()
        if fenceA[0] is not None:
            for e in ENGS:
                P.wait_only(e, fenceA[0])
        else:
            for e in ENGS:
                P.wait_only(e, fence)
        if DEBUG:
            dma("sync", dbg_oatt, OATT.rearrange("p c n -> p (c n)"), [], [], "dbg")
        if fenceA[0] is None:
            dma("gpsimd", PW, pool_w.rearrange("g c d -> c g d"), [], ["PW"], "w3_pw")
            dma("gpsimd", WUA, w_up_attn.rearrange("(c p) n -> p c n", p=128), [], ["WUA"], "w3_ua")
            dma("gpsimd", WUP, w_up_pool.rearrange("(c p) n -> p c n", p=128), [], ["WUP"], "w3_up")
            dma("gpsimd", WO, w_out.rearrange("(k p) n -> p k n", p=128), [], ["WO"], "w3_wo")

        def proj(col, n0, n):
            b = bank()
            for k in range(8):
                mm(ps[b][:, 0:n], W2[:, k, col:col + 128], xb3[:, k, n0:n0 + n], k == 0, k == 7, [("p3xb", k), w2key(col)], ["ps%d" % b])
            return b

        def gate2(b):
            i = thc[0] % 2
            thc[0] += 1
            act(TTt[i], ps[b][:, :], ACT.Silu, ["ps%d" % b], ["TT%d" % i])
            return TTt[i], "TT%d" % i

        if early[0] is None:
            p3_issue(0)
            p3_rstd(0)
        if fenceA[0] is None:
            p3_apply(0)
        bank_first.extend([4, 5, 6, 7, 2, 3])
        pb[0] = 0
        for T in range(4):
            if T + 1 < 4:
                p3_issue(T + 1)
            for blk in range(4):
                row0 = T * 512 + blk * 128
                dma("sync", RES[blk], x_own[row0:row0 + 128, :], [], [("RES", blk)], "res%d" % blk, extra=(fence if T == 0 else ()))
            def uproj(g):
                bm = proj(g * 128, 0, 512)
                bh = proj(g * 128, 512, 64)
                uk = "U%d" % g
                copy("scalar", U[g][:, :, 16:144], ps[bm][:, :].rearrange("p (b n) -> p b n", b=4), ["ps%d" % bm], [uk])
                copy("scalar", U[g][:, :, 0:16], ps[bh][:, 0:64].rearrange("p (b n) -> p b n", b=4), ["ps%d" % bh], [uk])

            def chain_pool(g):
                src, srck = U[g], "U%d" % g
                bufs = [(S1, "S1"), (S2, "S2")]
                w = 1
                lo = 0
                for lvl in range(g + 1):
                    dst, dstk = bufs[lvl % 2]
                    lo2 = lo + w
                    tt("gpsimd", dst[:, :, lo2:144], src[:, :, lo2:144], src[:, :, lo2 - w:144 - w], ALU.add, [srck], [dstk])
                    src, srck, lo, w = dst, dstk, lo2, 2 * w
                return src, srck, w

            def chain_fin(g, c3):
                src, srck, w = c3
                stt("vector", PL[g].rearrange("p (b n) -> p b n", b=4), U[g][:, :, 16:144], -float(w), src[:, :, 16:144], ALU.mult, ALU.add,
                    [srck, "U%d" % g], ["PL%d" % g])

            def zattn(c):
                bz = proj(1024 + c * 128, 0, 512)
                t2, t2k = gate2(bz)
                tt("vector", YA[:, c, :], OATT[:, c, T * 512:(T + 1) * 512], t2, ALU.mult, [t2k, ("OATT", c, T)], [("YA", c)])

            uproj(3)
            c3 = chain_pool(3)
            uproj(2)
            uproj(1)
            uproj(0)
            chain_fin(3, c3)
            c2 = chain_pool(2)
            zattn(0)
            zattn(1)
            chain_fin(2, c2)
            c1 = chain_pool(1)
            zattn(2)
            chain_fin(1, c1)
            c0 = chain_pool(0)
            zattn(3)
            chain_fin(0, c0)
            for g in (3, 2, 1, 0):
                bz = proj(512 + g * 128, 0, 512)
                t2, t2k = gate2(bz)
                by = bank()
                mm(ps[by][:, :], PW[:, g, :], PL[g], True, True, ["PL%d" % g, "PW"], ["ps%d" % by])
                stt("vector", YP[:, g, :], ps[by][:, :], hps[:, g:g + 1], t2, ALU.mult, ALU.mult, ["ps%d" % by, t2k, "hps"], [("YP", g)])
            for m in range(8):
                nxt = T + 1 < 4
                if nxt and m == 1:
                    rb = (bank(), bank())
                    reserved.update(rb)
                if nxt and 1 <= m <= 4:
                    p3_rstd(T + 1, banks=rb, ks=range(2 * (m - 1), 2 * m), do_mm=False, fin=False)
                bgp = proj(1536 + m * 128, 0, 512)
                bga = proj(2560 + m * 128, 0, 512)
                if nxt and 1 <= m <= 4:
                    p3_rstd(T + 1, banks=rb, ks=range(2 * (m - 1), 2 * m), do_sq=False, fin=False)
                bup = bank()
                for c in range(4):
                    mm(ps[bup][:, :], WUP[:, c, m * 128:(m + 1) * 128], YP[:, c, :], c == 0, c == 3, [("YP", c), "WUP"], ["ps%d" % bup])
                bua = bank()
                for c in range(4):
                    mm(ps[bua][:, :], WUA[:, c, m * 128:(m + 1) * 128], YA[:, c, :], c == 0, c == 3, [("YA", c), "WUA"], ["ps%d" % bua])
                i = thc[0] % 2
                thc[0] += 1
                act(TH[i], ps[bgp][:, :], ACT.Tanh, ["ps%d" % bgp], ["TH%d" % i], scale=0.5)
                stt("vector", M1, TH[i], 1.0, ps[bup][:, :], ALU.add, ALU.mult, ["TH%d" % i, "ps%d" % bup], ["M1"])
                i = thc[0] % 2
                thc[0] += 1
                act(TH[i], ps[bga][:, :], ACT.Tanh, ["ps%d" % bga], ["TH%d" % i], scale=0.5)
                stt("vector", M2, TH[i], 1.0, ps[bua][:, :], ALU.add, ALU.mult, ["TH%d" % i, "ps%d" % bua], ["M2"])
                tt("vector" if m == 7 else "gpsimd", MG[:, m, :], M1, M2, ALU.add, ["M1", "M2"], [("MG", m)])
                if m == 5 and nxt:
                    p3_rstd(T + 1, banks=rb, ks=(), fin=True)
                    reserved.clear()
            if T + 1 < 4:
                p3_apply(T + 1)
            memset("vector", ss4, 0.0, ["ss4"])
            for blk in range(4):
                row0 = T * 512 + blk * 128
                sk = ("ss4", blk)
                for hf in range(2):
                    bo = bank()
                    for m in range(8):
                        mm(ps[bo][:, :], MG[:, m, blk * 128:(blk + 1) * 128], WO[:, m, hf * 512:(hf + 1) * 512], m == 0, m == 7,
                           [("MG", m), "WO"], ["ps%d" % bo])
                    stt("vector", RES[blk][:, hf * 512:(hf + 1) * 512], ps[bo][:, :], 0.5, RES[blk][:, hf * 512:(hf + 1) * 512], ALU.mult, ALU.add,
                        ["ps%d" % bo, ("RES", blk)], [("RES", blk)])
                act(PLj[:, 0:1024], RES[blk], ACT.Square, [("RES", blk), "ss4"], ["PL0", "PL1", sk], accum=ss4[:, blk:blk + 1])
                act(ss4[:, blk:blk + 1], ss4[:, blk:blk + 1], ACT.Ln, [sk, "eps_c"], [sk], bias=eps_c, scale=1.0 / 1024)
                act(ss4[:, blk:blk + 1], ss4[:, blk:blk + 1], ACT.Exp, [sk], [sk], scale=-0.5)
                for pb_ in ([blk - 1] if blk > 0 else []) + ([blk] if blk == 3 else []):
                    r0_ = T * 512 + pb_ * 128
                    stt("vector", RES[pb_], RES[pb_], ss4[:, pb_:pb_ + 1], fng_b, ALU.mult, ALU.mult, [("RES", pb_), ("ss4", pb_), "fng"], [("RES", pb_)])
                    dma("sync", out[r0_:r0_ + 128, :], RES[pb_], [("RES", pb_)], [], "out%d" % pb_)

        fin = P.fence_set()
        P.wait_only("sync", fin)
        P.emit(nc)
    return nc


def _core_layout(j):
    a = [2 * s + ((s % 2) if j == 0 else 1 - (s % 2)) for s in range(16)]
    b = [2 * s + (1 - (s % 2) if j == 0 else (s % 2)) for s in range(16)]
    return a, b


def make_in_maps(x, meta_tokens, norm_g, w_in, b_forget, pool_w, pool_scale, w_up_pool, w_up_attn, w_out, final_norm_g):
    f = lambda v: np.ascontiguousarray(np.asarray(v, dtype=np.float32))
    x = f(x)
    meta = f(meta_tokens)
    shared = {
        "w_in": f(w_in)[0], "pool_w": f(pool_w)[0], "w_up_pool": f(w_up_pool)[0], "w_up_attn": f(w_up_attn)[0],
        "w_out": f(w_out)[0],
        "ng": np.ascontiguousarray(f(norm_g)[0].reshape(8, 128).T),
        "pscale": np.ascontiguousarray(f(pool_scale)[0].reshape(4, 128).T),
        "bfg": f(b_forget)[0], "fng": f(final_norm_g),
    }
    kk = np.arange(128)[:, None]
    qq = np.arange(128)[None, :]
    tri = np.where(kk <= qq, 0.0, NEG).astype(np.float32)
    in_maps = []
    for core in range(8):
        bidx, j = core // 2, core % 2
        a, b = _core_layout(j)
        seq = np.concatenate([meta, x[bidx]], 0)
        parts = [seq[16 + g * 128:16 + (g + 1) * 128] for g in a]
        parts += [seq[16 + g * 128:16 + (g + 1) * 128] for g in b]
        parts.append(seq[0:16])
        parts += [seq[g * 128:g * 128 + 16] for g in a]
        xloc = np.concatenate(parts, 0)
        gpos = np.array([1 + g for g in a] + [1 + g for g in b] + [0])
        pmat = (gpos[:, None] < gpos[None, :]).astype(np.float32)
        full = np.full((128, 128), NEG, np.float32)
        zero = np.zeros((128, 128), np.float32)
        mk = np.concatenate([tri, full if j == 0 else zero, zero if j == 0 else full], 1)
        m = dict(shared)
        m["xT"] = np.ascontiguousarray(xloc.T)
        m["x_own"] = np.ascontiguousarray(xloc[0:2048])
        m["pm"] = np.ascontiguousarray(pmat)
        m["masks"] = np.ascontiguousarray(mk)
        in_maps.append(m)
    return in_maps


def kernel(x, meta_tokens, norm_g, w_in, b_forget, pool_w, pool_scale, w_up_pool, w_up_attn, w_out, final_norm_g):
    in_maps = make_in_maps(x, meta_tokens, norm_g, w_in, b_forget, pool_w, pool_scale, w_up_pool, w_up_attn, w_out, final_norm_g)
    nc = build_nc()
    res = run_bass_kernel_spmd(nc, in_maps, core_ids=list(range(8)))
    outp = np.zeros((4, 4096, 1024), np.float32)
    for core in range(8):
        bidx, j = core // 2, core % 2
        a, _ = _core_layout(j)
        o = np.asarray(res.results[core]["out"], dtype=np.float32)
        for s, g in enumerate(a):
            outp[bidx, g * 128:(g + 1) * 128] = o[s * 128:(s + 1) * 128]
    return outp
```
